# Optimizing a Trainium2 kernel written in Bass

```python
import math
import jax, jax.numpy as jnp
from jax import lax
import numpy as np

D_MODEL = 2048
BATCH = 4
SEQ = 4096
DEPTH = 1

CTX_LEN = 256
GRID_W = 64
Q_BLOCK = 128
ROPE_THETA = 10000.0
EPS = 1e-6
ADA_CHUNKS = 6

A_HEADS = 16
A_KV_HEADS = 4
A_HEAD_DIM = 128
B_HEADS = 8
B_HEAD_DIM = 64
B_V_DIM = 2 * B_HEAD_DIM

A_Q = A_HEADS * A_HEAD_DIM
A_KV = A_KV_HEADS * A_HEAD_DIM
B_QK = B_HEADS * 2 * B_HEAD_DIM
B_V = B_HEADS * B_V_DIM
KV_WIDTH = 2 * A_KV + B_QK + B_V
IN_WIDTH = KV_WIDTH + A_Q + B_QK + 2 * D_MODEL

D_FF = -(-8 * D_MODEL // (3 * 256)) * 256

kernel_name = "hybrid_gqa_diffattn_prefix_dit_block"


def rms_norm(x, g):
    xf = x.astype(jnp.float32)
    y = xf * lax.rsqrt(jnp.mean(xf * xf, axis=-1, keepdims=True) + EPS)
    return (y * g.astype(jnp.float32)).astype(x.dtype)


def modulate(h, shift, scale):
    return h * (1 + scale) + shift


def adaln(cond, w_ada, b_ada):
    return jnp.split(jax.nn.silu(cond) @ w_ada + b_ada, ADA_CHUNKS, axis=-1)


def axial_rope(rows, head_dim):
    n_freq = head_dim // 4
    freqs = ROPE_THETA ** (-jnp.arange(n_freq, dtype=jnp.float32) / n_freq)
    row = jnp.repeat(jnp.arange(rows, dtype=jnp.float32), GRID_W)
    col = jnp.tile(jnp.arange(GRID_W, dtype=jnp.float32), rows)
    ang = jnp.concatenate([row[:, None] * freqs, col[:, None] * freqs], axis=-1)
    return jnp.cos(ang), jnp.sin(ang)


def apply_rope(x, cos, sin):
    half = x.shape[-1] // 2
    shape = (cos.shape[0],) + (1,) * (x.ndim - 3) + (half,)
    cos = cos.reshape(shape).astype(x.dtype)
    sin = sin.reshape(shape).astype(x.dtype)
    x1, x2 = x[..., :half], x[..., half:]
    return jnp.concatenate([x1 * cos - x2 * sin, x2 * cos + x1 * sin], axis=-1)


def split_kv(p_kv, k_norm_a, k_norm_b):
    b, n = p_kv.shape[:2]
    k_a, v_a, k_b, v_b = jnp.split(p_kv, [A_KV, 2 * A_KV, 2 * A_KV + B_QK], axis=-1)
    k_a = rms_norm(k_a.reshape(b, n, A_KV_HEADS, A_HEAD_DIM), k_norm_a)
    v_a = v_a.reshape(b, n, A_KV_HEADS, A_HEAD_DIM)
    k_b = rms_norm(k_b.reshape(b, n, B_HEADS, 2, B_HEAD_DIM), k_norm_b)
    v_b = v_b.reshape(b, n, B_HEADS, B_V_DIM)
    return k_a, v_a, k_b, v_b


def split_qg(p_qg, q_norm_a, q_norm_b):
    b, n = p_qg.shape[:2]
    q_a, q_b, g_a, g_b = jnp.split(p_qg, [A_Q, A_Q + B_QK, A_Q + B_QK + D_MODEL], axis=-1)
    q_a = rms_norm(q_a.reshape(b, n, A_HEADS, A_HEAD_DIM), q_norm_a)
    q_b = rms_norm(q_b.reshape(b, n, B_HEADS, 2, B_HEAD_DIM), q_norm_b)
    return q_a, q_b, g_a, g_b


def gqa(q, k, v):
    b, n, hq, d = q.shape
    hkv = k.shape[2]
    qg = q.reshape(b, n, hkv, hq // hkv, d)
    s = jnp.einsum('bnkgd,bmkd->bkgnm', qg, k).astype(jnp.float32) * (d ** -0.5)
    p = jax.nn.softmax(s, axis=-1).astype(v.dtype)
    o = jnp.einsum('bkgnm,bmkd->bnkgd', p, v)
    return o.reshape(b, n, hq * d)


def diff_attn(q, k, v, lam, lam_init, subln_g):
    b, n, h, _, dh = q.shape
    s = jnp.einsum('bnhid,bmhid->bhinm', q, k).astype(jnp.float32) * (dh ** -0.5)
    p = jax.nn.softmax(s, axis=-1)
    a = (p[:, :, 0] - lam * p[:, :, 1]).astype(v.dtype)
    o = jnp.einsum('bhnm,bmhe->bnhe', a, v)
    o = rms_norm(o, subln_g) * (1 - lam_init)
    return o.reshape(b, n, h * B_V_DIM)


def sweep_query_blocks(attn_fn, q):
    b, s = q.shape[:2]
    nblk = s // Q_BLOCK
    qb = jnp.moveaxis(q.reshape((b, nblk, Q_BLOCK) + q.shape[2:]), 1, 0)
    o = lax.map(attn_fn, qb)
    return jnp.moveaxis(o, 0, 1).reshape(b, s, o.shape[-1])


def merge_branches(o_a, o_b, g_a, g_b, w_br_a, w_br_b, w_out):
    merged = jax.nn.sigmoid(g_a) * (o_a @ w_br_a) + jax.nn.sigmoid(g_b) * (o_b @ w_br_b)
    return merged @ w_out


def swiglu_sublayer(x, shift, scale, gate, norm_g, w_ff_gate, w_ff_up, w_ff_down):
    h = modulate(rms_norm(x, norm_g), shift, scale)
    return x + gate * ((jax.nn.silu(h @ w_ff_gate) * (h @ w_ff_up)) @ w_ff_down)


def setup_inputs(seed: int = 0) -> dict:
    key = jax.random.key(seed)
    ks = jax.random.split(key, 24)
    f32 = jnp.float32
    L = DEPTH

    def w(k, shape, fan_in, s=1.0):
        return jax.random.normal(k, shape, f32) * (s * fan_in ** -0.5)

    def gain(k, shape):
        return 1.0 + 0.05 * jax.random.normal(k, shape, f32)

    return {
        'x': jax.random.normal(ks[0], (BATCH, SEQ, D_MODEL), f32),
        'c': jax.random.normal(ks[1], (BATCH, D_MODEL), f32),
        'ctx': jax.random.normal(ks[2], (BATCH, CTX_LEN, D_MODEL), f32),
        'c_ctx': jax.random.normal(ks[3], (D_MODEL,), f32),
        'w_ada': w(ks[4], (L, D_MODEL, ADA_CHUNKS * D_MODEL), D_MODEL, 0.5),
        'b_ada': 0.02 * jax.random.normal(ks[5], (L, ADA_CHUNKS * D_MODEL), f32),
        'norm1_g': gain(ks[6], (L, D_MODEL)),
        'w_in': w(ks[7], (L, D_MODEL, IN_WIDTH), D_MODEL),
        'q_norm_a': gain(ks[8], (L, A_HEAD_DIM)),
        'k_norm_a': gain(ks[9], (L, A_HEAD_DIM)),
        'q_norm_b': gain(ks[10], (L, B_HEAD_DIM)),
        'k_norm_b': gain(ks[11], (L, B_HEAD_DIM)),
        'lam_q1': 0.1 * jax.random.normal(ks[12], (L, B_HEAD_DIM), f32),
        'lam_k1': 0.1 * jax.random.normal(ks[13], (L, B_HEAD_DIM), f32),
        'lam_q2': 0.1 * jax.random.normal(ks[14], (L, B_HEAD_DIM), f32),
        'lam_k2': 0.1 * jax.random.normal(ks[15], (L, B_HEAD_DIM), f32),
        'subln_g': gain(ks[16], (L, B_V_DIM)),
        'w_br_a': w(ks[17], (L, A_Q, D_MODEL), A_Q),
        'w_br_b': w(ks[18], (L, B_V, D_MODEL), B_V),
        'w_out': w(ks[19], (L, D_MODEL, D_MODEL), D_MODEL),
        'norm2_g': gain(ks[20], (L, D_MODEL)),
        'w_ff_gate': w(ks[21], (L, D_MODEL, D_FF), D_MODEL),
        'w_ff_up': w(ks[22], (L, D_MODEL, D_FF), D_MODEL),
        'w_ff_down': w(ks[23], (L, D_FF, D_MODEL), D_FF),
    }


def reference(x, c, ctx, c_ctx, w_ada, b_ada, norm1_g, w_in, q_norm_a, k_norm_a, q_norm_b, k_norm_b,
              lam_q1, lam_k1, lam_q2, lam_k2, subln_g, w_br_a, w_br_b, w_out, norm2_g,
              w_ff_gate, w_ff_up, w_ff_down):
    rows = x.shape[1] // GRID_W
    cos_a, sin_a = axial_rope(rows, A_HEAD_DIM)
    cos_b, sin_b = axial_rope(rows, B_HEAD_DIM)

    for l in range(DEPTH):
        last = l == DEPTH - 1
        lam_init = 0.8 - 0.6 * math.exp(-0.3 * l)
        lam = (jnp.exp(jnp.sum(lam_q1[l].astype(jnp.float32) * lam_k1[l].astype(jnp.float32)))
               - jnp.exp(jnp.sum(lam_q2[l].astype(jnp.float32) * lam_k2[l].astype(jnp.float32)))
               + lam_init)
        m_lat = [m[:, None, :] for m in adaln(c, w_ada[l], b_ada[l])]
        m_ctx = adaln(c_ctx, w_ada[l], b_ada[l])
        w_in_l = w_in[l]

        hc = modulate(rms_norm(ctx, norm1_g[l]), m_ctx[0], m_ctx[1])
        pc = hc @ (w_in_l[:, :KV_WIDTH] if last else w_in_l)
        ka_c, va_c, kb_c, vb_c = split_kv(pc[..., :KV_WIDTH], k_norm_a[l], k_norm_b[l])

        h = modulate(rms_norm(x, norm1_g[l]), m_lat[0], m_lat[1])
        p = h @ w_in_l
        ka, va, kb, vb = split_kv(p[..., :KV_WIDTH], k_norm_a[l], k_norm_b[l])
        qa, qb, ga, gb = split_qg(p[..., KV_WIDTH:], q_norm_a[l], q_norm_b[l])
        qa = apply_rope(qa, cos_a, sin_a)
        ka = apply_rope(ka, cos_a, sin_a)
        qb = apply_rope(qb, cos_b, sin_b)
        kb = apply_rope(kb, cos_b, sin_b)

        ka_all = jnp.concatenate([ka_c, ka], axis=1)
        va_all = jnp.concatenate([va_c, va], axis=1)
        kb_all = jnp.concatenate([kb_c, kb], axis=1)
        vb_all = jnp.concatenate([vb_c, vb], axis=1)
        sg = subln_g[l]
        oa = sweep_query_blocks(lambda qblk: gqa(qblk, ka_all, va_all), qa)
        ob = sweep_query_blocks(lambda qblk: diff_attn(qblk, kb_all, vb_all, lam, lam_init, sg), qb)
        x_new = x + m_lat[2] * merge_branches(oa, ob, ga, gb, w_br_a[l], w_br_b[l], w_out[l])
        x_new = swiglu_sublayer(x_new, m_lat[3], m_lat[4], m_lat[5], norm2_g[l],
                                w_ff_gate[l], w_ff_up[l], w_ff_down[l])

        if not last:
            qa_c, qb_c, ga_c, gb_c = split_qg(pc[..., KV_WIDTH:], q_norm_a[l], q_norm_b[l])
            oa_c = gqa(qa_c, ka_c, va_c)
            ob_c = diff_attn(qb_c, kb_c, vb_c, lam, lam_init, sg)
            ctx = ctx + m_ctx[2] * merge_branches(oa_c, ob_c, ga_c, gb_c, w_br_a[l], w_br_b[l], w_out[l])
            ctx = swiglu_sublayer(ctx, m_ctx[3], m_ctx[4], m_ctx[5], norm2_g[l],
                                  w_ff_gate[l], w_ff_up[l], w_ff_down[l])
        x = x_new
    return x
```

```python
import contextlib
import os
import numpy as np
import concourse.bass as bass
import concourse.mybir as mybir
from concourse.bass_utils import run_bass_kernel_spmd

F32 = mybir.dt.float32
BF16 = mybir.dt.bfloat16
AF = mybir.ActivationFunctionType
ALU = mybir.AluOpType
AX = mybir.AxisListType

D = 2048
KC = 16
NKEY = 4352
NT = 34
NQ = 2048
NQT = 16
DFF = 5632
NFF = 44
EPS = 1e-6
GRID_W = 64
ENGS = ("pe", "act", "dve", "pool", "sp")


class Op:
    __slots__ = ("eng", "fn", "deps", "signal", "ticket", "is_dma", "sem", "semval", "waits", "idx", "relay")

    def __init__(self, eng, fn, is_dma):
        self.eng = eng
        self.fn = fn
        self.is_dma = is_dma
        self.deps = {}
        self.signal = False
        self.ticket = 0
        self.sem = None
        self.semval = 0
        self.waits = None
        self.relay = None


class Sched:
    def __init__(self, n_dma_sems=48):
        self.ops = {e: [] for e in ENGS}
        self.lastw = {}
        self.readers = {}
        self.n_dma_sems = n_dma_sems
        self.dma_last = [None] * n_dma_sems
        self.dma_cnt = [0] * n_dma_sems
        self.dma_rr = 0
        self.dma_rr_sw = 0
        self.n_hw_sems = n_dma_sems - 12
        self.nops = 0
        self.relay_fn = None
        self.fence = None
        self.pending_dma = []
        self.strict = bool(int(os.environ.get("SCHED_STRICT", "1")))

    def add(self, eng, fn, reads=(), writes=(), dma=False):
        op = Op(eng, fn, dma)
        op.idx = self.nops
        self.nops += 1
        deps = op.deps
        if self.fence is not None:
            deps[self.fence] = "raw"
        for k in reads:
            w = self.lastw.get(k)
            if w is not None:
                if w.is_dma and w.eng == "pool" and eng != "dve":
                    if w.relay is None:
                        w.relay = self.add("dve", self.relay_fn, reads=[k], writes=["_jk"])
                    w = w.relay
                deps[w] = "raw"
        for k in writes:
            w = self.lastw.get(k)
            if w is not None and w not in deps:
                deps[w] = "waw"
            rd = self.readers.get(k)
            if rd:
                for r in rd.values():
                    if r is not op and r not in deps:
                        deps[r] = "war"
        if dma:
            if eng == "pool":
                i = self.n_hw_sems + self.dma_rr_sw
                self.dma_rr_sw = (self.dma_rr_sw + 1) % (self.n_dma_sems - self.n_hw_sems)
            else:
                i = self.dma_rr
                self.dma_rr = (i + 1) % self.n_hw_sems
            prev = self.dma_last[i]
            if prev is not None:
                deps[prev] = "raw"
            self.dma_cnt[i] += 16
            op.sem = i
            op.semval = self.dma_cnt[i]
            self.dma_last[i] = op
            self.pending_dma.append(op)
        for k in reads:
            rd = self.readers.get(k)
            if rd is None:
                rd = self.readers[k] = {}
            rd[("d", op.idx) if dma else eng] = op
        for k in writes:
            self.lastw[k] = op
            self.readers[k] = {}
        self.ops[eng].append(op)
        return op

    def barrier(self):
        op = Op("dve", self.relay_fn, False)
        op.idx = self.nops
        self.nops += 1
        if self.fence is not None:
            op.deps[self.fence] = "raw"
        for e in ENGS:
            for o in reversed(self.ops[e]):
                if not o.is_dma and o.fn is not None:
                    op.deps[o] = "raw"
                    break
        for o in self.pending_dma:
            op.deps[o] = "raw"
        self.pending_dma = []
        self.ops["dve"].append(op)
        self.fence = op
        self.lastw = {"_jk": op}
        self.readers = {}
        return op

    def finalize(self):
        for e in ENGS:
            for op in self.ops[e]:
                need = []
                for p, kind in op.deps.items():
                    if p.is_dma:
                        need.append(p)
                    elif p.eng == op.eng and not op.is_dma:
                        if op.eng != "pe" and (kind == "raw" or self.strict):
                            need.append(p)
                    else:
                        need.append(p)
                for p in need:
                    if not p.is_dma:
                        p.signal = True
                op.waits = need
        for e in ENGS:
            t = 0
            for op in self.ops[e]:
                if op.signal and not op.is_dma:
                    t += 1
                    op.ticket = t

    def emit_one(self, e, eng, sems, dma_sems):
        waited = {}
        for op in self.ops[e]:
            wl = {}
            for p in op.waits:
                if p.is_dma:
                    key = ("d", p.sem)
                    val = p.semval
                else:
                    key = p.eng
                    val = p.ticket
                if waited.get(key, 0) >= val:
                    continue
                if wl.get(key, 0) < val:
                    wl[key] = val
            for key, val in wl.items():
                waited[key] = val
                s = dma_sems[key[1]] if isinstance(key, tuple) else sems[key]
                eng.wait_ge(s, val)
            if op.fn is None:
                continue
            ins = op.fn(eng)
            if op.is_dma:
                ins.then_inc(dma_sems[op.sem], 16)
            elif op.signal:
                ins.then_inc(sems[e], 1)


class Rot:
    def __init__(self, name, n):
        self.name = name
        self.n = n
        self.i = -1

    def next(self):
        self.i = (self.i + 1) % self.n
        return self.i

    def key(self, i=None):
        return f"{self.name}{self.i if i is None else i}"


def build_program(debug=0):
    nc = bass.Bass("TRN2", target_bir_lowering=False)
    S = Sched()

    def din(name, shape, dt=F32):
        return nc.dram_tensor(name, shape, dt, kind="ExternalInput").ap()

    def dscr(name, shape, dt=BF16):
        kind = "ExternalOutput" if debug else "Internal"
        return nc.dram_tensor(name, shape, dt, kind=kind).ap()

    xall = din("xall", [NKEY, D])
    rope_d = din("rope", [NKEY, 192])
    cT_d = din("cT", [128, 32])
    vecs_d = din("vecs", [128, 129])
    gains_d = din("gains", [128, 2048])
    lamv_d = din("lamv", [128, 256])
    ident_d = din("ident", [128, 128])
    w_ada = din("w_ada", [D, 12288])
    w_in = din("w_in", [D, 10240])
    w_br_a = din("w_br_a", [2048, D])
    w_br_b = din("w_br_b", [1024, D])
    w_out = din("w_out", [D, D])
    w_gate = din("w_ff_gate", [D, DFF])
    w_up = din("w_ff_up", [D, DFF])
    w_down = din("w_ff_down", [DFF, D])
    out_d = nc.dram_tensor("out", [NQ, D], F32, kind="ExternalOutput").ap()

    KTa = dscr("KTa", [4, 128, NKEY])
    Va = dscr("Va", [NKEY, 512])
    KTb = dscr("KTb", [8, 128, NKEY])
    Vb = dscr("Vb", [NKEY, 1024])
    QTa = dscr("QTa", [16, 128, NQ])
    QTb = dscr("QTb", [8, 128, NQ])
    sigA = dscr("sigA", [D, NQ])
    sigB = dscr("sigB", [D, NQ])
    xnew_d = dscr("xnew", [NQ, D], F32)

    identF = nc.alloc_sbuf_tensor("identF", [128, 128], F32)
    identB = nc.alloc_sbuf_tensor("identB", [128, 128], BF16)
    onesB = nc.alloc_sbuf_tensor("onesB", [128, 128], BF16)
    onesF = nc.alloc_sbuf_tensor("onesF", [128, 128], F32)
    jk = nc.alloc_sbuf_tensor("jk", [128, 8], F32)
    vecs = nc.alloc_sbuf_tensor("vecs_sb", [128, 129], F32)
    modT = nc.alloc_sbuf_tensor("modT", [128, 96, 2], F32)
    GB = nc.alloc_sbuf_tensor("GB", [128, 6, 16], F32)
    gbc = nc.alloc_sbuf_tensor("gbc", [128, 2, D], F32)
    lam = nc.alloc_sbuf_tensor("lam", [128, 8], F32)
    S.relay_fn = lambda e: e.memset(jk[:, 0:1], 0.0)

    def sp_load(dst, src, key):
        return S.add("sp", lambda e: e.dma_start(out=dst, in_=src), writes=[key], dma=True)

    def cast_load(dst, src, key):
        return S.add("pool", lambda e: e.dma_start(out=dst, in_=src), writes=[key], dma=True)

    sp_load(identF[:], ident_d, "identF")
    sp_load(vecs[:], vecs_d, "vecs")
    S.add("dve", lambda e: e.tensor_copy(identB[:], identF[:]), reads=["identF"], writes=["identB"])
    S.add("dve", lambda e: e.memset(onesB[:], 1.0), writes=["onesB"])
    S.add("dve", lambda e: e.memset(onesF[:], 1.0), writes=["onesF"])

    with contextlib.ExitStack() as ph:
        cTt = ph.enter_context(nc.sbuf_tensor("cTt", [128, 32], F32))
        sT = ph.enter_context(nc.sbuf_tensor("sT", [128, 32], BF16))
        lamt = ph.enter_context(nc.sbuf_tensor("lamt", [128, 256], F32))
        lamp = ph.enter_context(nc.sbuf_tensor("lamp", [128, 128], F32))
        gcol = ph.enter_context(nc.sbuf_tensor("gcol", [128, 2, 128], F32))
        wA = [ph.enter_context(nc.sbuf_tensor(f"wA{i}", [128, KC, 512], BF16)) for i in range(3)]
        ps_mod = ph.enter_context(nc.psum_tensor("ps_mod", [128, 192], F32))
        ps_g = [ph.enter_context(nc.psum_tensor(f"ps_g{i}", [128, 512], F32)) for i in range(2)]

        sp_load(cTt[:], cT_d, "cTt")
        sp_load(lamt[:], lamv_d, "lamt")
        S.add("act", lambda e: e.activation(out=sT[:], in_=cTt[:], func=AF.Silu), reads=["cTt"], writes=["sT"])
        wrot = Rot("wA", 3)
        for ng in range(24):
            b = wrot.next()
            cast_load(wA[b][:], w_ada[:, ng * 512:(ng + 1) * 512].rearrange("(kc p) n -> p kc n", p=128), wrot.key())
            for c4 in range(4):
                m = ng * 4 + c4
                for kc in range(KC):
                    S.add("pe", lambda e, b=b, c4=c4, kc=kc, m=m: e.matmul(
                        ps_mod[:, 2 * m:2 * m + 2], wA[b][:, kc, c4 * 128:(c4 + 1) * 128], sT[:, 2 * kc:2 * kc + 2],
                        start=(kc == 0), stop=(kc == KC - 1)),
                        reads=[wrot.key(), "sT"], writes=["ps_mod"])
        for j in range(2):
            S.add("dve", lambda e, j=j: e.tensor_tensor(
                modT[:, :, j], ps_mod[:, :].rearrange("p (m j) -> p m j", j=2)[:, :, j], vecs[:, 32:128], op=ALU.add),
                reads=["ps_mod", "vecs"], writes=[f"modT{j}"])
        for (dst, gsl, sc_chunk, sh_chunk, j) in ((0, 0, 1, 0, 0), (2, 0, 1, 0, 1), (4, 16, 4, 3, 0)):
            S.add("dve", lambda e, dst=dst, gsl=gsl, sc=sc_chunk, j=j: e.scalar_tensor_tensor(
                out=GB[:, dst, :], in0=modT[:, sc * 16:(sc + 1) * 16, j], scalar=1.0, in1=vecs[:, gsl:gsl + 16],
                op0=ALU.add, op1=ALU.mult), reads=[f"modT{j}", "vecs"], writes=[f"GB{dst}"])
            S.add("dve", lambda e, dst=dst, sh=sh_chunk, j=j: e.tensor_copy(
                GB[:, dst + 1, :], modT[:, sh * 16:(sh + 1) * 16, j]), reads=[f"modT{j}"], writes=[f"GB{dst + 1}"])
        for gi, chunk in ((0, 2), (1, 5)):
            for kc in range(KC):
                q = (gi * KC + kc) % 2
                S.add("dve", lambda e, q=q, chunk=chunk, kc=kc: e.tensor_scalar(
                    gcol[:, q, :], onesF[:], modT[:, chunk * 16 + kc, 0:1], None, op0=ALU.mult),
                    reads=["onesF", "modT0"], writes=[f"gcol{q}"])
                pb = (kc // 4) % 2
                S.add("pe", lambda e, q=q, kc=kc, pb=pb: e.transpose(
                    ps_g[pb][:, (kc % 4) * 128:(kc % 4 + 1) * 128], gcol[:, q, :], identF[:]),
                    reads=[f"gcol{q}", "identF"], writes=[f"ps_g{pb}"])
                if kc % 4 == 3:
                    S.add("dve", lambda e, gi=gi, kc=kc, pb=pb: e.tensor_copy(
                        gbc[:, gi, (kc - 3) * 128:(kc + 1) * 128], ps_g[pb][:]),
                        reads=[f"ps_g{pb}"], writes=[f"gbc{gi}"])
        S.add("dve", lambda e: e.tensor_tensor(lamp[:, 0:64], lamt[:, 0:64], lamt[:, 64:128], op=ALU.mult),
              reads=["lamt"], writes=["lamp0"])
        S.add("dve", lambda e: e.tensor_tensor(lamp[:, 64:128], lamt[:, 128:192], lamt[:, 192:256], op=ALU.mult),
              reads=["lamt"], writes=["lamp1"])
        S.add("dve", lambda e: e.tensor_reduce(out=lam[:, 2:4], in_=lamp[:, :].rearrange("p (a d) -> p a d", a=2),
                                               axis=AX.X, op=ALU.add), reads=["lamp0", "lamp1"], writes=["lam23"])
        S.add("act", lambda e: e.activation(out=lam[:, 4:6], in_=lam[:, 2:4], func=AF.Exp), reads=["lam23"], writes=["lam45"])
        S.add("dve", lambda e: e.scalar_tensor_tensor(out=lam[:, 0:1], in0=lam[:, 5:6], scalar=-0.2, in1=lam[:, 4:5],
                                                      op0=ALU.add, op1=ALU.subtract), reads=["lam45"], writes=["lam0"])
        S.add("dve", lambda e: e.tensor_scalar(lam[:, 1:2], vecs[:, 128:129], 0.8, None, op0=ALU.mult),
              reads=["vecs"], writes=["lam1"])
        S.barrier()

    IN_GROUPS = []
    for g in range(20):
        c0 = g * 512
        if g == 0:
            typ = "ka"
        elif g == 1:
            typ = "va"
        elif g in (2, 3):
            typ = "kb"
        elif g in (4, 5):
            typ = "vb"
        elif 6 <= g <= 9:
            typ = "qa"
        elif g in (10, 11):
            typ = "qb"
        elif 12 <= g <= 15:
            typ = "ga"
        else:
            typ = "gb"
        IN_GROUPS.append((c0, typ, g))

    with contextlib.ExitStack() as ph:
        HTW = 2304
        hT = ph.enter_context(nc.sbuf_tensor("hT", [128, KC, HTW], BF16))
        ropeT = ph.enter_context(nc.sbuf_tensor("ropeT", [128, NT, 192], F32))
        gains = ph.enter_context(nc.sbuf_tensor("gains_sb", [128, 2048], F32))
        xt = [ph.enter_context(nc.sbuf_tensor(f"xt{i}", [128, D], F32)) for i in range(2)]
        xn = ph.enter_context(nc.sbuf_tensor("xn", [128, D], F32))
        st = ph.enter_context(nc.sbuf_tensor("st", [128, 4, 16], F32))
        wI = [ph.enter_context(nc.sbuf_tensor(f"wI{i}", [128, KC, 512], BF16)) for i in range(2)]
        qraw = [ph.enter_context(nc.sbuf_tensor(f"qraw{i}", [128, 512], F32)) for i in range(2)]
        sq = ph.enter_context(nc.sbuf_tensor("sq", [128, 512], F32))
        qg = ph.enter_context(nc.sbuf_tensor("qg", [128, 512], F32))
        ma = ph.enter_context(nc.sbuf_tensor("ma", [128, 512], F32))
        mb = ph.enter_context(nc.sbuf_tensor("mb", [128, 512], F32))
        ro = ph.enter_context(nc.sbuf_tensor("ro", [128, 512], F32))
        qb16 = [ph.enter_context(nc.sbuf_tensor(f"qb16_{i}", [128, 512], BF16)) for i in range(2)]
        kT = [ph.enter_context(nc.sbuf_tensor(f"kT{i}", [128, 512], BF16)) for i in range(3)]
        vt = [ph.enter_context(nc.sbuf_tensor(f"vt{i}", [128, 512], BF16)) for i in range(3)]
        pst = [ph.enter_context(nc.psum_tensor(f"pst{i}", [128, 512], F32)) for i in range(4)]
        ppj = [ph.enter_context(nc.psum_tensor(f"ppj{i}", [128, 512], F32)) for i in range(2)]
        ptr = [ph.enter_context(nc.psum_tensor(f"ptr{i}", [128, 512], BF16)) for i in range(2)]

        sp_load(gains[:], gains_d, "gains")
        sp_load(ropeT[:], rope_d.rearrange("(t p) c -> p t c", p=128), "ropeT")

        xrot = Rot("xt", 2)

        def norm_tiles(tiles, gidx_fn):
            pend = None
            loads = {}
            for li, t in enumerate(tiles):
                pass
            def issue_load(li):
                t = tiles[li]
                b = xrot.next()
                sp_load(xt[b][:], xall[t * 128:(t + 1) * 128, :], xrot.key())
                loads[li] = b
            issue_load(0)
            for li, t in enumerate(tiles):
                if li + 1 < len(tiles):
                    issue_load(li + 1)
                b = loads[li]
                xk = f"xt{b}"
                gsel = gidx_fn(t)
                S.add("act", lambda e, b=b: e.activation(out=xn[:], in_=xt[b][:], func=AF.Square, accum_out=st[:, 0, 0:1]),
                      reads=[xk], writes=["xn", "st0"])
                S.add("dve", lambda e: e.tensor_scalar(st[:, 1, 0:1], st[:, 0, 0:1], 1.0 / D, EPS, op0=ALU.mult, op1=ALU.add),
                      reads=["st0"], writes=["st1"])
                S.add("act", lambda e: e.activation(out=st[:, 2, 0:1], in_=st[:, 1, 0:1], func=AF.Sqrt), reads=["st1"], writes=["st2"])
                S.add("dve", lambda e: e.reciprocal(st[:, 3, 0:1], st[:, 2, 0:1]), reads=["st2"], writes=["st3"])
                S.add("dve", lambda e, b=b: e.tensor_scalar(xn[:], xt[b][:], st[:, 3, 0:1], None, op0=ALU.mult),
                      reads=["st3", xk], writes=["xn"])
                for kc in range(KC):
                    pb = kc // 4
                    S.add("pe", lambda e, kc=kc, pb=pb: e.transpose(pst[pb][:, (kc % 4) * 128:(kc % 4 + 1) * 128],
                                                                    xn[:, kc * 128:(kc + 1) * 128], identF[:]),
                          reads=["xn", "identF"], writes=[f"pst{pb}"])
                for kc in range(KC):
                    pb = kc // 4
                    S.add("dve", lambda e, kc=kc, pb=pb, li=li, gsel=gsel: e.tensor_scalar(
                        hT[:, kc, li * 128:(li + 1) * 128], pst[pb][:, (kc % 4) * 128:(kc % 4 + 1) * 128],
                        GB[:, gsel, kc:kc + 1], GB[:, gsel + 1, kc:kc + 1], op0=ALU.mult, op1=ALU.add),
                        reads=[f"pst{pb}"], writes=[f"hT{li}"])

        TYPES = {
            "qa": (4, 128, 0, 0, 64),
            "ka": (4, 128, 512, 0, 64),
            "qb": (8, 64, 1024, 128, 160),
            "kb": (8, 64, 1536, 128, 160),
        }
        prot = Rot("ppj", 2)
        qrrot = Rot("qraw", 2)
        qbrot = Rot("qb16_", 2)
        trrot = Rot("ptr", 2)
        ktrot = Rot("kT", 3)
        vtrot = Rot("vt", 3)
        wirot = Rot("wI", 2)

        def qk_post(pb, t, li, typ, c0):
            nh, d, gc0, cc0, sc0 = TYPES[typ]
            half = d // 2
            pk = f"ppj{pb}"
            qi = qrrot.next()
            qk_ = qrrot.key()
            S.add("act", lambda e: e.activation(out=qraw[qi][:], in_=ppj[pb][:], func=AF.Copy), reads=[pk], writes=[qk_])
            S.add("act", lambda e: e.activation(out=sq[:], in_=qraw[qi][:], func=AF.Square), reads=[qk_], writes=["sq"])
            S.add("dve", lambda e: e.tensor_reduce(out=st[:, 0, 0:nh], in_=sq[:, :].rearrange("p (h d) -> p h d", h=nh),
                                                   axis=AX.X, op=ALU.add), reads=["sq"], writes=["st0"])
            S.add("dve", lambda e: e.tensor_scalar(st[:, 1, 0:nh], st[:, 0, 0:nh], 1.0 / d, EPS, op0=ALU.mult, op1=ALU.add),
                  reads=["st0"], writes=["st1"])
            S.add("act", lambda e: e.activation(out=st[:, 2, 0:nh], in_=st[:, 1, 0:nh], func=AF.Sqrt), reads=["st1"], writes=["st2"])
            S.add("dve", lambda e: e.tensor_tensor(qg[:], qraw[qi][:], gains[:, gc0:gc0 + 512], op=ALU.mult),
                  reads=[qk_, "gains"], writes=["qg"])
            v4 = "p (h two d) -> p h two d"
            cosb = ropeT[:, t, cc0:cc0 + half].unsqueeze(1).unsqueeze(1).broadcast_to([128, nh, 2, half])
            sinb = ropeT[:, t, sc0:sc0 + half].unsqueeze(1).unsqueeze(1).broadcast_to([128, nh, 2, half])
            S.add("dve", lambda e: e.tensor_tensor(ma[:, :].rearrange(v4, h=nh, two=2), qg[:, :].rearrange(v4, h=nh, two=2), cosb, op=ALU.mult),
                  reads=["qg", "ropeT"], writes=["ma"])
            S.add("dve", lambda e: e.tensor_tensor(mb[:, :].rearrange(v4, h=nh, two=2), qg[:, :].rearrange(v4, h=nh, two=2), sinb, op=ALU.mult),
                  reads=["qg", "ropeT"], writes=["mb"])
            rov = ro[:, :].rearrange(v4, h=nh, two=2)
            mav = ma[:, :].rearrange(v4, h=nh, two=2)
            mbv = mb[:, :].rearrange(v4, h=nh, two=2)
            S.add("dve", lambda e: e.tensor_tensor(rov[:, :, 0, :], mav[:, :, 0, :], mbv[:, :, 1, :], op=ALU.subtract),
                  reads=["ma", "mb"], writes=["ro0"])
            S.add("dve", lambda e: e.tensor_tensor(rov[:, :, 1, :], mav[:, :, 1, :], mbv[:, :, 0, :], op=ALU.add),
                  reads=["ma", "mb"], writes=["ro1"])
            bi = qbrot.next()
            bk = qbrot.key()
            S.add("dve", lambda e: e.reciprocal(st[:, 3, 0:nh], st[:, 2, 0:nh]), reads=["st2"], writes=["st3"])
            S.add("dve", lambda e: e.tensor_tensor(qb16[bi][:, :].rearrange("p (h d) -> p h d", h=nh),
                                                   ro[:, :].rearrange("p (h d) -> p h d", h=nh),
                                                   st[:, 3, 0:nh].unsqueeze(2).broadcast_to([128, nh, d]), op=ALU.mult),
                  reads=["ro0", "ro1", "st3"], writes=[bk])
            ti = trrot.next()
            tk = trrot.key()
            for c in range(4):
                S.add("pe", lambda e, c=c: e.transpose(ptr[ti][:, c * 128:(c + 1) * 128], qb16[bi][:, c * 128:(c + 1) * 128], identB[:]),
                      reads=[bk, "identB"], writes=[tk])
            ki = ktrot.next()
            kk = ktrot.key()
            S.add("act", lambda e: e.activation(out=kT[ki][:], in_=ptr[ti][:], func=AF.Copy), reads=[tk], writes=[kk])
            if typ == "ka":
                dst = KTa[:, :, t * 128:(t + 1) * 128]
            elif typ == "kb":
                h0 = (c0 - 1024) // 128
                dst = KTb[h0:h0 + 4, :, t * 128:(t + 1) * 128]
            elif typ == "qa":
                h0 = (c0 - 3072) // 128
                dst = QTa[h0:h0 + 4, :, t * 128:(t + 1) * 128]
            else:
                h0 = (c0 - 5120) // 128
                dst = QTb[h0:h0 + 4, :, t * 128:(t + 1) * 128]
            S.add("sp", lambda e: e.dma_start(out=dst.rearrange("h d k -> d h k"),
                                              in_=kT[ki][:, :].rearrange("p (h k) -> p h k", h=4)),
                  reads=[kk], writes=[f"scr_{typ}_{c0}_{t}"], dma=True)

        def v_post(pb, t, typ, c0):
            pk = f"ppj{pb}"
            vi = vtrot.next()
            vk = vtrot.key()
            S.add("act", lambda e: e.activation(out=vt[vi][:], in_=ppj[pb][:], func=AF.Copy), reads=[pk], writes=[vk])
            if typ == "va":
                dst = Va[t * 128:(t + 1) * 128, :]
            else:
                c = c0 - 2048
                dst = Vb[t * 128:(t + 1) * 128, c:c + 512]
            S.add("sp", lambda e: e.dma_start(out=dst, in_=vt[vi][:]), reads=[vk], writes=[f"scr_{typ}_{c0}_{t}"], dma=True)

        def g_post(pb, fc, tb, typ):
            pk = f"ppj{pb}"
            vi = vtrot.next()
            vk = vtrot.key()
            S.add("act", lambda e: e.activation(out=vt[vi][:], in_=ppj[pb][:], func=AF.Sigmoid), reads=[pk], writes=[vk])
            dstT = sigA if typ == "ga" else sigB
            dst = dstT[fc * 128:(fc + 1) * 128, tb * 512:(tb + 1) * 512]
            S.add("sp", lambda e: e.dma_start(out=dst, in_=vt[vi][:]), reads=[vk], writes=[f"scr_{typ}_{fc}_{tb}"], dma=True)

        pend = [None]

        def flush_pend():
            if pend[0] is not None:
                pend[0]()
                pend[0] = None

        def project(tiles, groups):
            for (c0, typ, g) in groups:
                wb = wirot.next()
                wk = wirot.key()
                cast_load(wI[wb][:], w_in[:, c0:c0 + 512].rearrange("(kc p) n -> p kc n", p=128), wk)
                if typ in ("ga", "gb"):
                    flush_pend()
                    base = 6144 if typ == "ga" else 8192
                    for c4 in range(4):
                        fc = (c0 - base) // 128 + c4
                        for tb in range(len(tiles) // 4):
                            pb = prot.next()
                            for kc in range(KC):
                                S.add("pe", lambda e, pb=pb, wb=wb, kc=kc, c4=c4, tb=tb: e.matmul(
                                    ppj[pb][:], wI[wb][:, kc, c4 * 128:(c4 + 1) * 128], hT[:, kc, tb * 512:(tb + 1) * 512],
                                    start=(kc == 0), stop=(kc == KC - 1)),
                                    reads=[wk] + [f"hT{tb * 4 + i}" for i in range(4)], writes=[f"ppj{pb}"])
                            g_post(pb, fc, tb, typ)
                    continue
                for li, t in enumerate(tiles):
                    pb = prot.next()
                    for kc in range(KC):
                        S.add("pe", lambda e, pb=pb, wb=wb, kc=kc, li=li: e.matmul(
                            ppj[pb][:], hT[:, kc, li * 128:(li + 1) * 128], wI[wb][:, kc, :],
                            start=(kc == 0), stop=(kc == KC - 1)),
                            reads=[wk, f"hT{li}"], writes=[f"ppj{pb}"])
                    flush_pend()
                    if typ in ("va", "vb"):
                        pend[0] = lambda pb=pb, t=t, typ=typ, c0=c0: v_post(pb, t, typ, c0)
                    else:
                        pend[0] = lambda pb=pb, t=t, li=li, typ=typ, c0=c0: qk_post(pb, t, li, typ, c0)

        tilesA = list(range(16, 34))
        norm_tiles(tilesA, lambda t: 0 if t < 32 else 2)
        project(tilesA, IN_GROUPS[0:6])
        tilesB = list(range(0, 16))
        norm_tiles(tilesB, lambda t: 0)
        project(tilesB, IN_GROUPS)
        flush_pend()
        S.barrier()

    if debug == 1:
        return finish(nc, S, out_d, None)

    with contextlib.ExitStack() as ph2:
        OTa = ph2.enter_context(nc.sbuf_tensor("OTa", [128, 16, NQ], BF16))
        OTb = ph2.enter_context(nc.sbuf_tensor("OTb", [128, 8, NQ], BF16))
        if os.environ.get("DEBUG_MEMSET"):
            S.add("dve", lambda e: e.memset(OTa[:], 0.0), writes=["OTa_all"])
            S.add("dve", lambda e: e.memset(OTb[:], 0.0), writes=["OTb_all"])
        with contextlib.ExitStack() as ph:
            KTs = [ph.enter_context(nc.sbuf_tensor(f"KTs{i}", [128, NKEY], BF16)) for i in range(2)]
            Vs = [ph.enter_context(nc.sbuf_tensor(f"Vs{i}", [128, NT, 128], BF16)) for i in range(2)]
            QTs = [ph.enter_context(nc.sbuf_tensor(f"QTs{i}", [128, NQ], BF16)) for i in range(2)]
            PT = [ph.enter_context(nc.sbuf_tensor(f"PT{i}", [128, 512], BF16)) for i in range(4)]
            rinv = [ph.enter_context(nc.sbuf_tensor(f"rinv{i}", [128, 512], F32)) for i in range(2)]
            t1 = [ph.enter_context(nc.sbuf_tensor(f"t1_{i}", [128, 512], F32)) for i in range(2)]
            ob = ph.enter_context(nc.sbuf_tensor("ob", [128, 512], F32))
            sqb = ph.enter_context(nc.sbuf_tensor("sqb", [128, 512], BF16))
            rs = ph.enter_context(nc.sbuf_tensor("rs", [128, 3, 512], F32))
            pS = [ph.enter_context(nc.psum_tensor(f"pS{i}", [128, 512], F32)) for i in range(3)]
            pO = [ph.enter_context(nc.psum_tensor(f"pO{i}", [128, 512], F32)) for i in range(2)]
            pR = [ph.enter_context(nc.psum_tensor(f"pR{i}", [128, 512], F32)) for i in range(2)]
            pN = ph.enter_context(nc.psum_tensor("pN", [128, 512], F32))

            kvrot = Rot("kv", 2)
            qrot = Rot("QTs", 2)
            srot = Rot("pS", 3)
            ptrot = Rot("PT", 4)

            blocks = []
            acc = 0
            for g in range(int(os.environ.get('ATT_A', '4'))):
                for hq in range(4):
                    for qb in range(4):
                        blocks.append(dict(kind="a", g=g, h=4 * g + hq, qb=qb, lo=0, hi=128, ai=acc % 2,
                                           scale=128 ** -0.5, newkv=(hq == 0 and qb == 0), newq=(qb == 0)))
                        acc += 1
            for h in range(int(os.environ.get('ATT_B', '8'))):
                for qb in range(4):
                    for sub in range(2):
                        blocks.append(dict(kind="b", h=h, qb=qb, sub=sub, lo=sub * 64, hi=sub * 64 + 64, ai=sub,
                                           scale=64 ** -0.5, newkv=(qb == 0 and sub == 0), newq=(qb == 0 and sub == 0)))
            cur = {"kvb": 0, "qi": 0}

            def prefetch(blk):
                if blk["newkv"]:
                    kvb = kvrot.next()
                    cur["kvb"] = kvb
                    if blk["kind"] == "a":
                        g = blk["g"]
                        sp_load(KTs[kvb][:], KTa[g], f"KTs{kvb}")
                        sp_load(Vs[kvb][:], Va[:, g * 128:(g + 1) * 128].rearrange("(t p) d -> p t d", p=128), f"Vs{kvb}")
                    else:
                        h = blk["h"]
                        sp_load(KTs[kvb][:], KTb[h], f"KTs{kvb}")
                        sp_load(Vs[kvb][:], Vb[:, h * 128:(h + 1) * 128].rearrange("(t p) d -> p t d", p=128), f"Vs{kvb}")
                if blk["newq"]:
                    qi = qrot.next()
                    cur["qi"] = qi
                    src = QTa[blk["h"]] if blk["kind"] == "a" else QTb[blk["h"]]
                    sp_load(QTs[qi][:], src, f"QTs{qi}")
                blk["kvb"] = cur["kvb"]
                blk["qi"] = cur["qi"]

            def post_a(blk):
                ai, h, qb = blk["ai"], blk["h"], blk["qb"]
                S.add("dve", lambda e: e.reciprocal(rinv[ai][:], pR[ai][:]), reads=[f"pR{ai}"], writes=[f"rinv{ai}"])
                S.add("dve", lambda e: e.tensor_tensor(OTa[:, h, qb * 512:(qb + 1) * 512], pO[ai][:], rinv[ai][:], op=ALU.mult),
                      reads=[f"pO{ai}", f"rinv{ai}"], writes=[f"OTa{h}_{qb}"])

            def post_b1(blk):
                sub = blk["sub"]
                S.add("dve", lambda e: e.reciprocal(rinv[sub][:], pR[sub][:]), reads=[f"pR{sub}"], writes=[f"rinv{sub}"])
                S.add("dve", lambda e: e.tensor_tensor(t1[sub][:], pO[sub][:], rinv[sub][:], op=ALU.mult),
                      reads=[f"pO{sub}", f"rinv{sub}"], writes=[f"t1_{sub}"])
                if sub == 1:
                    S.add("dve", lambda e: e.scalar_tensor_tensor(out=ob[:], in0=t1[1][:], scalar=lam[:, 0:1], in1=t1[0][:],
                                                                  op0=ALU.mult, op1=ALU.add),
                          reads=["t1_0", "t1_1"], writes=["ob"])
                    S.add("dve", lambda e: e.tensor_tensor(sqb[:], ob[:], ob[:], op=ALU.mult), reads=["ob"], writes=["sqb"])

            def post_b2(blk):
                S.add("pe", lambda e: e.matmul(pN[:], onesB[:], sqb[:], start=True, stop=True), reads=["sqb"], writes=["pN"])
                S.add("dve", lambda e: e.tensor_scalar(rs[:, 0, :], pN[:], 1.0 / 128, EPS, op0=ALU.mult, op1=ALU.add),
                      reads=["pN"], writes=["rs0"])

            def post_b3(blk):
                h, qb = blk["h"], blk["qb"]
                S.add("act", lambda e: e.activation(out=rs[:, 1, :], in_=rs[:, 0, :], func=AF.Sqrt), reads=["rs0"], writes=["rs1"])
                S.add("dve", lambda e: e.reciprocal(rs[:, 2, :], rs[:, 1, :]), reads=["rs1"], writes=["rs2"])
                S.add("dve", lambda e: e.tensor_tensor(ob[:], ob[:], rs[:, 2, :], op=ALU.mult), reads=["ob", "rs2"], writes=["ob"])
                S.add("dve", lambda e: e.tensor_scalar(OTb[:, h, qb * 512:(qb + 1) * 512], ob[:], lam[:, 1:2], None, op0=ALU.mult),
                      reads=["ob"], writes=[f"OTb{h}_{qb}"])

            items = [(bi, kt) for bi in range(len(blocks)) for kt in range(NT)]
            LOOK = 2
            sidx = {}
            deferred = []

            def issue_S(i):
                bi, kt = items[i]
                blk = blocks[bi]
                if kt == 0:
                    if bi == 0:
                        prefetch(blk)
                    if bi + 1 < len(blocks):
                        prefetch(blocks[bi + 1])
                si = srot.next()
                sidx[i] = si
                kvb, qi, lo, hi, qb = blk["kvb"], blk["qi"], blk["lo"], blk["hi"], blk["qb"]
                S.add("pe", lambda e: e.matmul(
                    pS[si][:], KTs[kvb][lo:hi, kt * 128:(kt + 1) * 128], QTs[qi][lo:hi, qb * 512:(qb + 1) * 512],
                    start=True, stop=True), reads=[f"KTs{kvb}", f"QTs{qi}"], writes=[f"pS{si}"])

            for i in range(min(LOOK, len(items))):
                issue_S(i)
            for i in range(len(items)):
                if i + LOOK < len(items):
                    issue_S(i + LOOK)
                bi, kt = items[i]
                blk = blocks[bi]
                si = sidx.pop(i)
                pi = ptrot.next()
                pk = ptrot.key()
                kvb, ai, scale = blk["kvb"], blk["ai"], blk["scale"]
                S.add("act", lambda e, si=si, pi=pi, scale=scale: e.activation(out=PT[pi][:], in_=pS[si][:], func=AF.Exp, scale=scale),
                      reads=[f"pS{si}"], writes=[pk])
                S.add("pe", lambda e, pi=pi, kt=kt, kvb=kvb, ai=ai: e.matmul(pO[ai][:], Vs[kvb][:, kt, :], PT[pi][:],
                                                                          start=(kt == 0), stop=(kt == NT - 1)),
                      reads=[f"Vs{kvb}", pk], writes=[f"pO{ai}"])
                S.add("pe", lambda e, pi=pi, kt=kt, ai=ai: e.matmul(pR[ai][:], onesB[:], PT[pi][:],
                                                                 start=(kt == 0), stop=(kt == NT - 1)),
                      reads=[pk], writes=[f"pR{ai}"])
                for dd in [d for d in deferred if d[0] <= i]:
                    dd[1]()
                deferred = [d for d in deferred if d[0] > i]
                if kt == NT - 1:
                    if blk["kind"] == "a":
                        post_a(blk)
                    else:
                        post_b1(blk)
                        if blk["sub"] == 1:
                            deferred.append((i + 3, lambda blk=blk: post_b2(blk)))
                            deferred.append((i + 6, lambda blk=blk: post_b3(blk)))
            for dd in deferred:
                dd[1]()
            S.barrier()

        if debug == 2:
            return finish(nc, S, out_d, ("OT", OTa, OTb))

        with contextlib.ExitStack() as ph:
            mT = ph.enter_context(nc.sbuf_tensor("mT", [128, KC, 512], BF16))
            wBa = [ph.enter_context(nc.sbuf_tensor(f"wBa{i}", [128, 16, 256], BF16)) for i in range(2)]
            wBb = [ph.enter_context(nc.sbuf_tensor(f"wBb{i}", [128, 8, 256], BF16)) for i in range(2)]
            wO = [ph.enter_context(nc.sbuf_tensor(f"wO{i}", [128, KC, 512], BF16)) for i in range(2)]
            sg = [ph.enter_context(nc.sbuf_tensor(f"sg{i}", [128, 2, 512], BF16)) for i in range(3)]
            m1 = ph.enter_context(nc.sbuf_tensor("m1", [128, 512], F32))
            m2 = ph.enter_context(nc.sbuf_tensor("m2", [128, 512], F32))
            xp = [ph.enter_context(nc.sbuf_tensor(f"xp{i}", [128, 512], F32)) for i in range(3)]
            tmp = ph.enter_context(nc.sbuf_tensor("tmp3", [128, 512], F32))
            pA = [ph.enter_context(nc.psum_tensor(f"pA{i}", [128, 512], F32)) for i in range(2)]
            pB = [ph.enter_context(nc.psum_tensor(f"pB{i}", [128, 512], F32)) for i in range(2)]
            pX = [ph.enter_context(nc.psum_tensor(f"pX{i}", [128, 512], F32)) for i in range(2)]
            wbrot = Rot("wB", 2)
            worot = Rot("wO", 2)
            sgrot = Rot("sg", 3)
            xprot = Rot("xp", 3)
            for tb in range(4):
                for fg in range(8 if os.environ.get('P3A_BR', '1') == '1' else 0):
                    wb = wbrot.next()
                    cast_load(wBa[wb][:], w_br_a[:, fg * 256:(fg + 1) * 256].rearrange("(h p) n -> p h n", p=128), f"wBa{wb}")
                    cast_load(wBb[wb][:], w_br_b[:, fg * 256:(fg + 1) * 256].rearrange("(h p) n -> p h n", p=128), f"wBb{wb}")
                    for c2 in range(2):
                        fc = fg * 2 + c2
                        pi = fc % 2
                        si = sgrot.next()
                        sp_load(sg[si][:, 0, :], sigA[fc * 128:(fc + 1) * 128, tb * 512:(tb + 1) * 512], f"sgA{si}")
                        sp_load(sg[si][:, 1, :], sigB[fc * 128:(fc + 1) * 128, tb * 512:(tb + 1) * 512], f"sgB{si}")
                        for h in range(16):
                            S.add("pe", lambda e, pi=pi, wb=wb, h=h, c2=c2, tb=tb: e.matmul(
                                pA[pi][:], wBa[wb][:, h, c2 * 128:(c2 + 1) * 128], OTa[:, h, tb * 512:(tb + 1) * 512],
                                start=(h == 0), stop=(h == 15)), reads=[f"wBa{wb}"], writes=[f"pA{pi}"])
                        for h in range(8):
                            S.add("pe", lambda e, pi=pi, wb=wb, h=h, c2=c2, tb=tb: e.matmul(
                                pB[pi][:], wBb[wb][:, h, c2 * 128:(c2 + 1) * 128], OTb[:, h, tb * 512:(tb + 1) * 512],
                                start=(h == 0), stop=(h == 7)), reads=[f"wBb{wb}"], writes=[f"pB{pi}"])
                        S.add("dve", lambda e, pi=pi, si=si: e.tensor_tensor(m1[:], pA[pi][:], sg[si][:, 0, :], op=ALU.mult),
                              reads=[f"pA{pi}", f"sgA{si}"], writes=["m1"])
                        S.add("dve", lambda e, pi=pi, si=si: e.tensor_tensor(m2[:], pB[pi][:], sg[si][:, 1, :], op=ALU.mult),
                              reads=[f"pB{pi}", f"sgB{si}"], writes=["m2"])
                        S.add("dve", lambda e, fc=fc: e.tensor_tensor(mT[:, fc, :], m1[:], m2[:], op=ALU.add),
                              reads=["m1", "m2"], writes=[f"mT{fc}"])
                for fg in range(4 if os.environ.get('P3A_OUT', '1') == '1' else 0):
                    wo = worot.next()
                    cast_load(wO[wo][:], w_out[:, fg * 512:(fg + 1) * 512].rearrange("(kc p) n -> p kc n", p=128), f"wO{wo}")
                    for tt in range(4):
                        t = tb * 4 + tt
                        xi = xprot.next()
                        sp_load(xp[xi][:], xall[t * 128:(t + 1) * 128, fg * 512:(fg + 1) * 512], f"xp{xi}")
                        pi = (fg * 4 + tt) % 2
                        for kc in range(KC):
                            S.add("pe", lambda e, pi=pi, wo=wo, kc=kc, tt=tt: e.matmul(
                                pX[pi][:], mT[:, kc, tt * 128:(tt + 1) * 128], wO[wo][:, kc, :],
                                start=(kc == 0), stop=(kc == KC - 1)),
                                reads=[f"wO{wo}", f"mT{kc}"], writes=[f"pX{pi}"])
                        S.add("dve", lambda e, pi=pi, fg=fg: e.tensor_tensor(tmp[:], pX[pi][:], gbc[:, 0, fg * 512:(fg + 1) * 512], op=ALU.mult),
                              reads=[f"pX{pi}"], writes=["tmp3"])
                        S.add("dve", lambda e, xi=xi: e.tensor_tensor(xp[xi][:], xp[xi][:], tmp[:], op=ALU.add),
                              reads=["tmp3", f"xp{xi}"], writes=[f"xp{xi}"])
                        S.add("sp", lambda e, xi=xi, t=t, fg=fg: e.dma_start(out=xnew_d[t * 128:(t + 1) * 128, fg * 512:(fg + 1) * 512], in_=xp[xi][:]),
                              reads=[f"xp{xi}"], writes=[f"xnew_{t}_{fg}"], dma=True)
            S.barrier()

    if debug == 3:
        return finish(nc, S, out_d, None)

    with contextlib.ExitStack() as ph:
        aT = ph.enter_context(nc.sbuf_tensor("aT", [128, NFF, 512], BF16))
        h2T = ph.enter_context(nc.sbuf_tensor("h2T", [128, KC, 512], BF16))
        xw = [ph.enter_context(nc.sbuf_tensor(f"xw{i}", [128, D], F32)) for i in range(2)]
        xn2 = ph.enter_context(nc.sbuf_tensor("xn2", [128, D], F32))
        junk2 = ph.enter_context(nc.sbuf_tensor("junk2", [128, D], BF16))
        st2 = ph.enter_context(nc.sbuf_tensor("st2", [128, 4], F32))
        wG = [ph.enter_context(nc.sbuf_tensor(f"wG{i}", [128, KC, 256], BF16)) for i in range(2)]
        wU = [ph.enter_context(nc.sbuf_tensor(f"wU{i}", [128, KC, 256], BF16)) for i in range(2)]
        wD = [ph.enter_context(nc.sbuf_tensor(f"wD{i}", [128, NFF, 256], BF16)) for i in range(2)]
        sgl = [ph.enter_context(nc.sbuf_tensor(f"sgl{i}", [128, 512], F32)) for i in range(2)]
        xr = [ph.enter_context(nc.sbuf_tensor(f"xr{i}", [128, 256], F32)) for i in range(3)]
        tmp4 = ph.enter_context(nc.sbuf_tensor("tmp4", [128, 256], F32))
        pst2 = [ph.enter_context(nc.psum_tensor(f"pst2_{i}", [128, 512], F32)) for i in range(2)]
        pG = [ph.enter_context(nc.psum_tensor(f"pG{i}", [128, 512], F32)) for i in range(2)]
        pU = [ph.enter_context(nc.psum_tensor(f"pU{i}", [128, 512], F32)) for i in range(2)]
        pD = [ph.enter_context(nc.psum_tensor(f"pD{i}", [128, 256], F32)) for i in range(2)]
        xwrot = Rot("xw", 2)
        gurot = Rot("wGU", 2)
        wdrot = Rot("wD", 2)
        xrrot = Rot("xr", 3)
        slrot = Rot("sgl", 2)
        out_ops = []
        for tb in range(4):
            for tt in range(4):
                t = tb * 4 + tt
                b = xwrot.next()
                xk = xwrot.key()
                S.add("sp", lambda e, b=b, t=t: e.dma_start(out=xw[b][:], in_=xnew_d[t * 128:(t + 1) * 128, :]),
                      reads=[f"xnew_{t}_{fg}" for fg in range(4)], writes=[xk], dma=True)
                S.add("act", lambda e, b=b: e.activation(out=junk2[:], in_=xw[b][:], func=AF.Square, accum_out=st2[:, 0:1]),
                      reads=[xk], writes=["junk2", "s0"])
                S.add("dve", lambda e: e.tensor_scalar(st2[:, 1:2], st2[:, 0:1], 1.0 / D, EPS, op0=ALU.mult, op1=ALU.add),
                      reads=["s0"], writes=["s1"])
                S.add("act", lambda e: e.activation(out=st2[:, 2:3], in_=st2[:, 1:2], func=AF.Sqrt), reads=["s1"], writes=["s2"])
                S.add("dve", lambda e: e.reciprocal(st2[:, 3:4], st2[:, 2:3]), reads=["s2"], writes=["s3"])
                S.add("dve", lambda e, b=b: e.tensor_scalar(xn2[:], xw[b][:], st2[:, 3:4], None, op0=ALU.mult),
                      reads=["s3", xk], writes=["xn2"])
                for kc in range(KC):
                    pb = (kc // 4) % 2
                    S.add("pe", lambda e, kc=kc, pb=pb: e.transpose(pst2[pb][:, (kc % 4) * 128:(kc % 4 + 1) * 128],
                                                                    xn2[:, kc * 128:(kc + 1) * 128], identF[:]),
                          reads=["xn2", "identF"], writes=[f"pst2_{pb}"])
                    S.add("dve", lambda e, kc=kc, pb=pb, tt=tt: e.tensor_scalar(
                        h2T[:, kc, tt * 128:(tt + 1) * 128], pst2[pb][:, (kc % 4) * 128:(kc % 4 + 1) * 128],
                        GB[:, 4, kc:kc + 1], GB[:, 5, kc:kc + 1], op0=ALU.mult, op1=ALU.add),
                        reads=[f"pst2_{pb}"], writes=[f"h2T{tt}"])
            for fg in range(22):
                gb_ = gurot.next()
                cast_load(wG[gb_][:], w_gate[:, fg * 256:(fg + 1) * 256].rearrange("(kc p) n -> p kc n", p=128), f"wG{gb_}")
                cast_load(wU[gb_][:], w_up[:, fg * 256:(fg + 1) * 256].rearrange("(kc p) n -> p kc n", p=128), f"wU{gb_}")
                for c2 in range(2):
                    ffc = fg * 2 + c2
                    pi = ffc % 2
                    for kc in range(KC):
                        S.add("pe", lambda e, pi=pi, gb_=gb_, kc=kc, c2=c2: e.matmul(
                            pG[pi][:], wG[gb_][:, kc, c2 * 128:(c2 + 1) * 128], h2T[:, kc, :],
                            start=(kc == 0), stop=(kc == KC - 1)),
                            reads=[f"wG{gb_}"] + [f"h2T{i}" for i in range(4)], writes=[f"pG{pi}"])
                    for kc in range(KC):
                        S.add("pe", lambda e, pi=pi, gb_=gb_, kc=kc, c2=c2: e.matmul(
                            pU[pi][:], wU[gb_][:, kc, c2 * 128:(c2 + 1) * 128], h2T[:, kc, :],
                            start=(kc == 0), stop=(kc == KC - 1)),
                            reads=[f"wU{gb_}"] + [f"h2T{i}" for i in range(4)], writes=[f"pU{pi}"])
                    sl = slrot.next()
                    S.add("act", lambda e, pi=pi, sl=sl: e.activation(out=sgl[sl][:], in_=pG[pi][:], func=AF.Silu),
                          reads=[f"pG{pi}"], writes=[f"sgl{sl}"])
                    S.add("dve", lambda e, pi=pi, sl=sl, ffc=ffc: e.tensor_tensor(aT[:, ffc, :], pU[pi][:], sgl[sl][:], op=ALU.mult),
                          reads=[f"pU{pi}", f"sgl{sl}"], writes=[f"aT{ffc}"])
            for fg in range(8):
                wd = wdrot.next()
                cast_load(wD[wd][:], w_down[:, fg * 256:(fg + 1) * 256].rearrange("(c p) n -> p c n", p=128), f"wD{wd}")
                for tt in range(4):
                    t = tb * 4 + tt
                    xi = xrrot.next()
                    S.add("sp", lambda e, xi=xi, t=t, fg=fg: e.dma_start(out=xr[xi][:], in_=xnew_d[t * 128:(t + 1) * 128, fg * 256:(fg + 1) * 256]),
                          reads=[f"xnew_{t}_{fg // 2}"], writes=[f"xr{xi}"], dma=True)
                    pi = (fg * 4 + tt) % 2
                    for c in range(NFF):
                        S.add("pe", lambda e, pi=pi, wd=wd, c=c, tt=tt: e.matmul(
                            pD[pi][:], aT[:, c, tt * 128:(tt + 1) * 128], wD[wd][:, c, :],
                            start=(c == 0), stop=(c == NFF - 1)),
                            reads=[f"wD{wd}", f"aT{c}"], writes=[f"pD{pi}"])
                    S.add("dve", lambda e, pi=pi, fg=fg: e.tensor_tensor(tmp4[:], pD[pi][:], gbc[:, 1, fg * 256:(fg + 1) * 256], op=ALU.mult),
                          reads=[f"pD{pi}"], writes=["tmp4"])
                    S.add("dve", lambda e, xi=xi: e.tensor_tensor(xr[xi][:], xr[xi][:], tmp4[:], op=ALU.add),
                          reads=["tmp4", f"xr{xi}"], writes=[f"xr{xi}"])
                    out_ops.append(S.add("sp", lambda e, xi=xi, t=t, fg=fg: e.dma_start(
                        out=out_d[t * 128:(t + 1) * 128, fg * 256:(fg + 1) * 256], in_=xr[xi][:]),
                        reads=[f"xr{xi}"], writes=[f"out_{t}_{fg}"], dma=True))
    return finish(nc, S, out_d, None)


def finish(nc, S, out_d, extra):
    fin = Op("sp", None, False)
    fin.idx = S.nops
    for o in S.pending_dma:
        fin.deps[o] = "raw"
    if S.fence is not None:
        fin.deps[S.fence] = "raw"
    S.ops["sp"].append(fin)
    S.finalize()
    with contextlib.ExitStack() as st:
        sems = {e: st.enter_context(nc.semaphore(f"s_{e}")) for e in ENGS}
        dsems = [st.enter_context(nc.semaphore(f"d_{i}")) for i in range(S.n_dma_sems)]
        block = st.enter_context(nc.Block())

        @block.tensor
        def _(e):
            S.emit_one("pe", e, sems, dsems)

        @block.scalar
        def _(e):
            S.emit_one("act", e, sems, dsems)

        @block.vector
        def _(e):
            S.emit_one("dve", e, sems, dsems)

        @block.gpsimd
        def _(e):
            S.emit_one("pool", e, sems, dsems)

        @block.sync
        def _(e):
            S.emit_one("sp", e, sems, dsems)
    return nc


def _rope_tables(n_tokens_tile_positions):
    pos = n_tokens_tile_positions.astype(np.float32)
    row = np.floor(pos / GRID_W).astype(np.float32)
    col = (pos - row * GRID_W).astype(np.float32)
    outs = []
    for hd in (128, 64):
        nf = hd // 4
        freqs = (10000.0 ** (-np.arange(nf, dtype=np.float32) / nf)).astype(np.float32)
        ang = np.concatenate([row[:, None] * freqs, col[:, None] * freqs], axis=-1).astype(np.float32)
        outs.append((np.cos(ang).astype(np.float32), np.sin(ang).astype(np.float32)))
    return outs


def make_in_maps(x, c, ctx, c_ctx, w_ada, b_ada, norm1_g, w_in, q_norm_a, k_norm_a, q_norm_b, k_norm_b,
                 lam_q1, lam_k1, lam_q2, lam_k2, subln_g, w_br_a, w_br_b, w_out, norm2_g,
                 w_ff_gate, w_ff_up, w_ff_down):
    f = lambda a: np.ascontiguousarray(np.asarray(a, dtype=np.float32))
    x = f(x); c = f(c); ctx = f(ctx); c_ctx = f(c_ctx)
    shared = {
        "w_ada": f(w_ada)[0], "w_in": f(w_in)[0], "w_br_a": f(w_br_a)[0], "w_br_b": f(w_br_b)[0],
        "w_out": f(w_out)[0], "w_ff_gate": f(w_ff_gate)[0], "w_ff_up": f(w_ff_up)[0], "w_ff_down": f(w_ff_down)[0],
        "ident": np.eye(128, dtype=np.float32),
    }
    fm = lambda v: np.ascontiguousarray(f(v).reshape(-1, 128).T)
    vecs = np.concatenate([fm(norm1_g[0]), fm(norm2_g[0]), fm(b_ada[0]), f(subln_g[0]).reshape(128, 1)], axis=1)
    gains_row = np.concatenate([np.tile(f(q_norm_a[0]), 4), np.tile(f(k_norm_a[0]), 4),
                                np.tile(f(q_norm_b[0]), 8), np.tile(f(k_norm_b[0]), 8)])
    gains = np.ascontiguousarray(np.broadcast_to(gains_row[None, :], (128, 2048)))
    lam_row = np.concatenate([f(lam_q1[0]), f(lam_k1[0]), f(lam_q2[0]), f(lam_k2[0])])
    lamv = np.ascontiguousarray(np.broadcast_to(lam_row[None, :], (128, 256)))
    shared.update({"vecs": np.ascontiguousarray(vecs), "gains": gains, "lamv": lamv})
    (cA, sA), (cB, sB) = _rope_tables(np.arange(4096))
    rope_lat = np.concatenate([cA, sA, cB, sB], axis=1).astype(np.float32)
    rope_ctx = np.concatenate([np.ones((256, 64)), np.zeros((256, 64)), np.ones((256, 32)), np.zeros((256, 32))],
                              axis=1).astype(np.float32)
    in_maps = []
    for core in range(8):
        b, h = core // 2, core % 2
        own = slice(h * 2048, (h + 1) * 2048)
        oth = slice((1 - h) * 2048, (2 - h) * 2048)
        xall = np.concatenate([x[b, own], x[b, oth], ctx[b]], axis=0)
        rope = np.concatenate([rope_lat[own], rope_lat[oth], rope_ctx], axis=0)
        cpair = np.stack([c[b], c_ctx], axis=0)
        cT = np.ascontiguousarray(cpair.reshape(2, 16, 128).transpose(2, 1, 0).reshape(128, 32))
        m = dict(shared)
        m.update({"xall": np.ascontiguousarray(xall), "rope": np.ascontiguousarray(rope), "cT": cT})
        in_maps.append(m)
    return in_maps


_NC_CACHE = {}


def kernel(**inputs):
    in_maps = make_in_maps(**inputs)
    if "nc" not in _NC_CACHE:
        _NC_CACHE["nc"] = build_program(0)
    nc = _NC_CACHE["nc"]
    res = run_bass_kernel_spmd(nc, in_maps, core_ids=list(range(8)))
    out = np.zeros((4, 4096, 2048), dtype=np.float32)
    for core in range(8):
        b, h = core // 2, core % 2
        out[b, h * 2048:(h + 1) * 2048, :] = res.results[core]["out"]
    return out
```

```python
import contextlib
import os
import numpy as np
import concourse.bass as bass
import concourse.mybir as mybir
from concourse.bass_utils import run_bass_kernel_spmd

F32 = mybir.dt.float32
BF16 = mybir.dt.bfloat16
AF = mybir.ActivationFunctionType
ALU = mybir.AluOpType
AX = mybir.AxisListType

D = 2048
KC = 16
NKEY = 4352
NT = 34
NQ = 2048
NQT = 16
DFF = 5632
NFF = 44
EPS = 1e-6
GRID_W = 64
ENGS = ("pe", "act", "dve", "pool", "sp")


class Op:
    __slots__ = ("eng", "fn", "deps", "signal", "ticket", "is_dma", "sem", "semval", "waits", "idx", "relay")

    def __init__(self, eng, fn, is_dma):
        self.eng = eng
        self.fn = fn
        self.is_dma = is_dma
        self.deps = {}
        self.signal = False
        self.ticket = 0
        self.sem = None
        self.semval = 0
        self.waits = None
        self.relay = None


class Sched:
    def __init__(self, n_dma_sems=48):
        self.ops = {e: [] for e in ENGS}
        self.lastw = {}
        self.readers = {}
        self.n_dma_sems = n_dma_sems
        self.dma_last = [None] * n_dma_sems
        self.dma_cnt = [0] * n_dma_sems
        self.dma_rr = 0
        self.dma_rr_sw = 0
        self.n_hw_sems = n_dma_sems - 12
        self.nops = 0
        self.relay_fn = None
        self.fence = None
        self.pending_dma = []
        self.strict = bool(int(os.environ.get("SCHED_STRICT", "1")))

    def add(self, eng, fn, reads=(), writes=(), dma=False):
        op = Op(eng, fn, dma)
        op.idx = self.nops
        self.nops += 1
        deps = op.deps
        if self.fence is not None:
            deps[self.fence] = "raw"
        for k in reads:
            w = self.lastw.get(k)
            if w is not None:
                if w.is_dma and w.eng == "pool" and eng != "dve":
                    if w.relay is None:
                        w.relay = self.add("dve", self.relay_fn, reads=[k], writes=["_jk"])
                    w = w.relay
                deps[w] = "raw"
        for k in writes:
            w = self.lastw.get(k)
            if w is not None and w not in deps:
                deps[w] = "waw"
            rd = self.readers.get(k)
            if rd:
                for r in rd.values():
                    if r is not op and r not in deps:
                        deps[r] = "war"
        if dma:
            if eng == "pool":
                i = self.n_hw_sems + self.dma_rr_sw
                self.dma_rr_sw = (self.dma_rr_sw + 1) % (self.n_dma_sems - self.n_hw_sems)
            else:
                i = self.dma_rr
                self.dma_rr = (i + 1) % self.n_hw_sems
            prev = self.dma_last[i]
            if prev is not None:
                deps[prev] = "raw"
            self.dma_cnt[i] += 16
            op.sem = i
            op.semval = self.dma_cnt[i]
            self.dma_last[i] = op
            self.pending_dma.append(op)
        for k in reads:
            rd = self.readers.get(k)
            if rd is None:
                rd = self.readers[k] = {}
            rd[("d", op.idx) if dma else eng] = op
        for k in writes:
            self.lastw[k] = op
            self.readers[k] = {}
        self.ops[eng].append(op)
        return op

    def barrier(self):
        op = Op("dve", self.relay_fn, False)
        op.idx = self.nops
        self.nops += 1
        if self.fence is not None:
            op.deps[self.fence] = "raw"
        for e in ENGS:
            for o in reversed(self.ops[e]):
                if not o.is_dma and o.fn is not None:
                    op.deps[o] = "raw"
                    break
        for o in self.pending_dma:
            op.deps[o] = "raw"
        self.pending_dma = []
        self.ops["dve"].append(op)
        self.fence = op
        self.lastw = {"_jk": op}
        self.readers = {}
        return op

    def finalize(self):
        for e in ENGS:
            for op in self.ops[e]:
                need = []
                for p, kind in op.deps.items():
                    if p.is_dma:
                        need.append(p)
                    elif p.eng == op.eng and not op.is_dma:
                        if op.eng != "pe" and (kind == "raw" or self.strict):
                            need.append(p)
                    else:
                        need.append(p)
                for p in need:
                    if not p.is_dma:
                        p.signal = True
                op.waits = need
        for e in ENGS:
            t = 0
            for op in self.ops[e]:
                if op.signal and not op.is_dma:
                    t += 1
                    op.ticket = t

    def emit_one(self, e, eng, sems, dma_sems):
        waited = {}
        for op in self.ops[e]:
            wl = {}
            for p in op.waits:
                if p.is_dma:
                    key = ("d", p.sem)
                    val = p.semval
                else:
                    key = p.eng
                    val = p.ticket
                if waited.get(key, 0) >= val:
                    continue
                if wl.get(key, 0) < val:
                    wl[key] = val
            for key, val in wl.items():
                waited[key] = val
                s = dma_sems[key[1]] if isinstance(key, tuple) else sems[key]
                eng.wait_ge(s, val)
            if op.fn is None:
                continue
            ins = op.fn(eng)
            if op.is_dma:
                ins.then_inc(dma_sems[op.sem], 16)
            elif op.signal:
                ins.then_inc(sems[e], 1)


class Rot:
    def __init__(self, name, n):
        self.name = name
        self.n = n
        self.i = -1

    def next(self):
        self.i = (self.i + 1) % self.n
        return self.i

    def key(self, i=None):
        return f"{self.name}{self.i if i is None else i}"


def build_program(debug=0):
    nc = bass.Bass("TRN2", target_bir_lowering=False)
    S = Sched()

    def din(name, shape, dt=F32):
        return nc.dram_tensor(name, shape, dt, kind="ExternalInput").ap()

    def dscr(name, shape, dt=BF16):
        kind = "ExternalOutput" if debug else "Internal"
        return nc.dram_tensor(name, shape, dt, kind=kind).ap()

    xall = din("xall", [NKEY, D])
    rope_d = din("rope", [NKEY, 192])
    cT_d = din("cT", [128, 32])
    vecs_d = din("vecs", [128, 129])
    gains_d = din("gains", [128, 2048])
    lamv_d = din("lamv", [128, 256])
    ident_d = din("ident", [128, 128])
    w_ada = din("w_ada", [D, 12288])
    w_in = din("w_in", [D, 10240])
    w_br_a = din("w_br_a", [2048, D])
    w_br_b = din("w_br_b", [1024, D])
    w_out = din("w_out", [D, D])
    w_gate = din("w_ff_gate", [D, DFF])
    w_up = din("w_ff_up", [D, DFF])
    w_down = din("w_ff_down", [DFF, D])
    out_d = nc.dram_tensor("out", [NQ, D], F32, kind="ExternalOutput").ap()

    KTa = dscr("KTa", [4, 128, NKEY])
    Va = dscr("Va", [NKEY, 512])
    KTb = dscr("KTb", [8, 128, NKEY])
    Vb = dscr("Vb", [NKEY, 1024])
    QTa = dscr("QTa", [16, 128, NQ])
    QTb = dscr("QTb", [8, 128, NQ])
    sigA = dscr("sigA", [D, NQ])
    sigB = dscr("sigB", [D, NQ])
    xnew_d = dscr("xnew", [NQ, D], F32)

    identF = nc.alloc_sbuf_tensor("identF", [128, 128], F32)
    identB = nc.alloc_sbuf_tensor("identB", [128, 128], BF16)
    onesB = nc.alloc_sbuf_tensor("onesB", [128, 128], BF16)
    onesF = nc.alloc_sbuf_tensor("onesF", [128, 128], F32)
    jk = nc.alloc_sbuf_tensor("jk", [128, 8], F32)
    vecs = nc.alloc_sbuf_tensor("vecs_sb", [128, 129], F32)
    modT = nc.alloc_sbuf_tensor("modT", [128, 96, 2], F32)
    GB = nc.alloc_sbuf_tensor("GB", [128, 6, 16], F32)
    gbc = nc.alloc_sbuf_tensor("gbc", [128, 2, D], F32)
    lam = nc.alloc_sbuf_tensor("lam", [128, 8], F32)
    S.relay_fn = lambda e: e.memset(jk[:, 0:1], 0.0)

    def sp_load(dst, src, key):
        return S.add("sp", lambda e: e.dma_start(out=dst, in_=src), writes=[key], dma=True)

    def cast_load(dst, src, key):
        return S.add("pool", lambda e: e.dma_start(out=dst, in_=src), writes=[key], dma=True)

    sp_load(identF[:], ident_d, "identF")
    sp_load(vecs[:], vecs_d, "vecs")
    S.add("dve", lambda e: e.tensor_copy(identB[:], identF[:]), reads=["identF"], writes=["identB"])
    S.add("dve", lambda e: e.memset(onesB[:], 1.0), writes=["onesB"])
    S.add("dve", lambda e: e.memset(onesF[:], 1.0), writes=["onesF"])

    with contextlib.ExitStack() as ph:
        cTt = ph.enter_context(nc.sbuf_tensor("cTt", [128, 32], F32))
        sT = ph.enter_context(nc.sbuf_tensor("sT", [128, 32], BF16))
        lamt = ph.enter_context(nc.sbuf_tensor("lamt", [128, 256], F32))
        lamp = ph.enter_context(nc.sbuf_tensor("lamp", [128, 128], F32))
        gcol = ph.enter_context(nc.sbuf_tensor("gcol", [128, 2, 128], F32))
        wA = [ph.enter_context(nc.sbuf_tensor(f"wA{i}", [128, KC, 512], BF16)) for i in range(3)]
        ps_mod = ph.enter_context(nc.psum_tensor("ps_mod", [128, 192], F32))
        ps_g = [ph.enter_context(nc.psum_tensor(f"ps_g{i}", [128, 512], F32)) for i in range(2)]

        sp_load(cTt[:], cT_d, "cTt")
        sp_load(lamt[:], lamv_d, "lamt")
        S.add("act", lambda e: e.activation(out=sT[:], in_=cTt[:], func=AF.Silu), reads=["cTt"], writes=["sT"])
        wrot = Rot("wA", 3)
        for ng in range(24):
            b = wrot.next()
            cast_load(wA[b][:], w_ada[:, ng * 512:(ng + 1) * 512].rearrange("(kc p) n -> p kc n", p=128), wrot.key())
            for c4 in range(4):
                m = ng * 4 + c4
                for kc in range(KC):
                    S.add("pe", lambda e, b=b, c4=c4, kc=kc, m=m: e.matmul(
                        ps_mod[:, 2 * m:2 * m + 2], wA[b][:, kc, c4 * 128:(c4 + 1) * 128], sT[:, 2 * kc:2 * kc + 2],
                        start=(kc == 0), stop=(kc == KC - 1)),
                        reads=[wrot.key(), "sT"], writes=["ps_mod"])
        for j in range(2):
            S.add("dve", lambda e, j=j: e.tensor_tensor(
                modT[:, :, j], ps_mod[:, :].rearrange("p (m j) -> p m j", j=2)[:, :, j], vecs[:, 32:128], op=ALU.add),
                reads=["ps_mod", "vecs"], writes=[f"modT{j}"])
        for (dst, gsl, sc_chunk, sh_chunk, j) in ((0, 0, 1, 0, 0), (2, 0, 1, 0, 1), (4, 16, 4, 3, 0)):
            S.add("dve", lambda e, dst=dst, gsl=gsl, sc=sc_chunk, j=j: e.scalar_tensor_tensor(
                out=GB[:, dst, :], in0=modT[:, sc * 16:(sc + 1) * 16, j], scalar=1.0, in1=vecs[:, gsl:gsl + 16],
                op0=ALU.add, op1=ALU.mult), reads=[f"modT{j}", "vecs"], writes=[f"GB{dst}"])
            S.add("dve", lambda e, dst=dst, sh=sh_chunk, j=j: e.tensor_copy(
                GB[:, dst + 1, :], modT[:, sh * 16:(sh + 1) * 16, j]), reads=[f"modT{j}"], writes=[f"GB{dst + 1}"])
        for gi, chunk in ((0, 2), (1, 5)):
            for kc in range(KC):
                q = (gi * KC + kc) % 2
                S.add("dve", lambda e, q=q, chunk=chunk, kc=kc: e.tensor_scalar(
                    gcol[:, q, :], onesF[:], modT[:, chunk * 16 + kc, 0:1], None, op0=ALU.mult),
                    reads=["onesF", "modT0"], writes=[f"gcol{q}"])
                pb = (kc // 4) % 2
                S.add("pe", lambda e, q=q, kc=kc, pb=pb: e.transpose(
                    ps_g[pb][:, (kc % 4) * 128:(kc % 4 + 1) * 128], gcol[:, q, :], identF[:]),
                    reads=[f"gcol{q}", "identF"], writes=[f"ps_g{pb}"])
                if kc % 4 == 3:
                    S.add("dve", lambda e, gi=gi, kc=kc, pb=pb: e.tensor_copy(
                        gbc[:, gi, (kc - 3) * 128:(kc + 1) * 128], ps_g[pb][:]),
                        reads=[f"ps_g{pb}"], writes=[f"gbc{gi}"])
        S.add("dve", lambda e: e.tensor_tensor(lamp[:, 0:64], lamt[:, 0:64], lamt[:, 64:128], op=ALU.mult),
              reads=["lamt"], writes=["lamp0"])
        S.add("dve", lambda e: e.tensor_tensor(lamp[:, 64:128], lamt[:, 128:192], lamt[:, 192:256], op=ALU.mult),
              reads=["lamt"], writes=["lamp1"])
        S.add("dve", lambda e: e.tensor_reduce(out=lam[:, 2:4], in_=lamp[:, :].rearrange("p (a d) -> p a d", a=2),
                                               axis=AX.X, op=ALU.add), reads=["lamp0", "lamp1"], writes=["lam23"])
        S.add("act", lambda e: e.activation(out=lam[:, 4:6], in_=lam[:, 2:4], func=AF.Exp), reads=["lam23"], writes=["lam45"])
        S.add("dve", lambda e: e.scalar_tensor_tensor(out=lam[:, 0:1], in0=lam[:, 5:6], scalar=-0.2, in1=lam[:, 4:5],
                                                      op0=ALU.add, op1=ALU.subtract), reads=["lam45"], writes=["lam0"])
        S.add("dve", lambda e: e.tensor_scalar(lam[:, 1:2], vecs[:, 128:129], 0.8, None, op0=ALU.mult),
              reads=["vecs"], writes=["lam1"])
        S.barrier()

    IN_GROUPS = []
    for g in range(20):
        c0 = g * 512
        if g == 0:
            typ = "ka"
        elif g == 1:
            typ = "va"
        elif g in (2, 3):
            typ = "kb"
        elif g in (4, 5):
            typ = "vb"
        elif 6 <= g <= 9:
            typ = "qa"
        elif g in (10, 11):
            typ = "qb"
        elif 12 <= g <= 15:
            typ = "ga"
        else:
            typ = "gb"
        IN_GROUPS.append((c0, typ, g))

    with contextlib.ExitStack() as ph:
        HTW = 2304
        hT = ph.enter_context(nc.sbuf_tensor("hT", [128, KC, HTW], BF16))
        ropeT = ph.enter_context(nc.sbuf_tensor("ropeT", [128, NT, 192], F32))
        gains = ph.enter_context(nc.sbuf_tensor("gains_sb", [128, 2048], F32))
        xt = [ph.enter_context(nc.sbuf_tensor(f"xt{i}", [128, D], F32)) for i in range(2)]
        xn = ph.enter_context(nc.sbuf_tensor("xn", [128, D], F32))
        st = ph.enter_context(nc.sbuf_tensor("st", [128, 4, 16], F32))
        wI = [ph.enter_context(nc.sbuf_tensor(f"wI{i}", [128, KC, 512], BF16)) for i in range(2)]
        qraw = [ph.enter_context(nc.sbuf_tensor(f"qraw{i}", [128, 512], F32)) for i in range(2)]
        sq = ph.enter_context(nc.sbuf_tensor("sq", [128, 512], F32))
        qg = ph.enter_context(nc.sbuf_tensor("qg", [128, 512], F32))
        ma = ph.enter_context(nc.sbuf_tensor("ma", [128, 512], F32))
        mb = ph.enter_context(nc.sbuf_tensor("mb", [128, 512], F32))
        ro = ph.enter_context(nc.sbuf_tensor("ro", [128, 512], F32))
        qb16 = [ph.enter_context(nc.sbuf_tensor(f"qb16_{i}", [128, 512], BF16)) for i in range(2)]
        kT = [ph.enter_context(nc.sbuf_tensor(f"kT{i}", [128, 512], BF16)) for i in range(3)]
        vt = [ph.enter_context(nc.sbuf_tensor(f"vt{i}", [128, 512], BF16)) for i in range(3)]
        pst = [ph.enter_context(nc.psum_tensor(f"pst{i}", [128, 512], F32)) for i in range(4)]
        ppj = [ph.enter_context(nc.psum_tensor(f"ppj{i}", [128, 512], F32)) for i in range(2)]
        ptr = [ph.enter_context(nc.psum_tensor(f"ptr{i}", [128, 512], BF16)) for i in range(2)]

        sp_load(gains[:], gains_d, "gains")
        sp_load(ropeT[:], rope_d.rearrange("(t p) c -> p t c", p=128), "ropeT")

        xrot = Rot("xt", 2)

        def norm_tiles(tiles, gidx_fn):
            pend = None
            loads = {}
            for li, t in enumerate(tiles):
                pass
            def issue_load(li):
                t = tiles[li]
                b = xrot.next()
                sp_load(xt[b][:], xall[t * 128:(t + 1) * 128, :], xrot.key())
                loads[li] = b
            issue_load(0)
            for li, t in enumerate(tiles):
                if li + 1 < len(tiles):
                    issue_load(li + 1)
                b = loads[li]
                xk = f"xt{b}"
                gsel = gidx_fn(t)
                S.add("act", lambda e, b=b: e.activation(out=xn[:], in_=xt[b][:], func=AF.Square, accum_out=st[:, 0, 0:1]),
                      reads=[xk], writes=["xn", "st0"])
                S.add("dve", lambda e: e.tensor_scalar(st[:, 1, 0:1], st[:, 0, 0:1], 1.0 / D, EPS, op0=ALU.mult, op1=ALU.add),
                      reads=["st0"], writes=["st1"])
                S.add("act", lambda e: e.activation(out=st[:, 2, 0:1], in_=st[:, 1, 0:1], func=AF.Sqrt), reads=["st1"], writes=["st2"])
                S.add("dve", lambda e: e.reciprocal(st[:, 3, 0:1], st[:, 2, 0:1]), reads=["st2"], writes=["st3"])
                S.add("dve", lambda e, b=b: e.tensor_scalar(xn[:], xt[b][:], st[:, 3, 0:1], None, op0=ALU.mult),
                      reads=["st3", xk], writes=["xn"])
                for kc in range(KC):
                    pb = kc // 4
                    S.add("pe", lambda e, kc=kc, pb=pb: e.transpose(pst[pb][:, (kc % 4) * 128:(kc % 4 + 1) * 128],
                                                                    xn[:, kc * 128:(kc + 1) * 128], identF[:]),
                          reads=["xn", "identF"], writes=[f"pst{pb}"])
                for kc in range(KC):
                    pb = kc // 4
                    S.add("dve", lambda e, kc=kc, pb=pb, li=li, gsel=gsel: e.tensor_scalar(
                        hT[:, kc, li * 128:(li + 1) * 128], pst[pb][:, (kc % 4) * 128:(kc % 4 + 1) * 128],
                        GB[:, gsel, kc:kc + 1], GB[:, gsel + 1, kc:kc + 1], op0=ALU.mult, op1=ALU.add),
                        reads=[f"pst{pb}"], writes=[f"hT{li}"])

        TYPES = {
            "qa": (4, 128, 0, 0, 64),
            "ka": (4, 128, 512, 0, 64),
            "qb": (8, 64, 1024, 128, 160),
            "kb": (8, 64, 1536, 128, 160),
        }
        prot = Rot("ppj", 2)
        qrrot = Rot("qraw", 2)
        qbrot = Rot("qb16_", 2)
        trrot = Rot("ptr", 2)
        ktrot = Rot("kT", 3)
        vtrot = Rot("vt", 3)
        wirot = Rot("wI", 2)

        def qk_post(pb, t, li, typ, c0):
            nh, d, gc0, cc0, sc0 = TYPES[typ]
            half = d // 2
            pk = f"ppj{pb}"
            qi = qrrot.next()
            qk_ = qrrot.key()
            S.add("act", lambda e: e.activation(out=qraw[qi][:], in_=ppj[pb][:], func=AF.Copy), reads=[pk], writes=[qk_])
            S.add("act", lambda e: e.activation(out=sq[:], in_=qraw[qi][:], func=AF.Square), reads=[qk_], writes=["sq"])
            S.add("dve", lambda e: e.tensor_reduce(out=st[:, 0, 0:nh], in_=sq[:, :].rearrange("p (h d) -> p h d", h=nh),
                                                   axis=AX.X, op=ALU.add), reads=["sq"], writes=["st0"])
            S.add("dve", lambda e: e.tensor_scalar(st[:, 1, 0:nh], st[:, 0, 0:nh], 1.0 / d, EPS, op0=ALU.mult, op1=ALU.add),
                  reads=["st0"], writes=["st1"])
            S.add("act", lambda e: e.activation(out=st[:, 2, 0:nh], in_=st[:, 1, 0:nh], func=AF.Sqrt), reads=["st1"], writes=["st2"])
            S.add("dve", lambda e: e.tensor_tensor(qg[:], qraw[qi][:], gains[:, gc0:gc0 + 512], op=ALU.mult),
                  reads=[qk_, "gains"], writes=["qg"])
            v4 = "p (h two d) -> p h two d"
            cosb = ropeT[:, t, cc0:cc0 + half].unsqueeze(1).unsqueeze(1).broadcast_to([128, nh, 2, half])
            sinb = ropeT[:, t, sc0:sc0 + half].unsqueeze(1).unsqueeze(1).broadcast_to([128, nh, 2, half])
            S.add("dve", lambda e: e.tensor_tensor(ma[:, :].rearrange(v4, h=nh, two=2), qg[:, :].rearrange(v4, h=nh, two=2), cosb, op=ALU.mult),
                  reads=["qg", "ropeT"], writes=["ma"])
            S.add("dve", lambda e: e.tensor_tensor(mb[:, :].rearrange(v4, h=nh, two=2), qg[:, :].rearrange(v4, h=nh, two=2), sinb, op=ALU.mult),
                  reads=["qg", "ropeT"], writes=["mb"])
            rov = ro[:, :].rearrange(v4, h=nh, two=2)
            mav = ma[:, :].rearrange(v4, h=nh, two=2)
            mbv = mb[:, :].rearrange(v4, h=nh, two=2)
            S.add("dve", lambda e: e.tensor_tensor(rov[:, :, 0, :], mav[:, :, 0, :], mbv[:, :, 1, :], op=ALU.subtract),
                  reads=["ma", "mb"], writes=["ro0"])
            S.add("dve", lambda e: e.tensor_tensor(rov[:, :, 1, :], mav[:, :, 1, :], mbv[:, :, 0, :], op=ALU.add),
                  reads=["ma", "mb"], writes=["ro1"])
            bi = qbrot.next()
            bk = qbrot.key()
            S.add("dve", lambda e: e.reciprocal(st[:, 3, 0:nh], st[:, 2, 0:nh]), reads=["st2"], writes=["st3"])
            S.add("dve", lambda e: e.tensor_tensor(qb16[bi][:, :].rearrange("p (h d) -> p h d", h=nh),
                                                   ro[:, :].rearrange("p (h d) -> p h d", h=nh),
                                                   st[:, 3, 0:nh].unsqueeze(2).broadcast_to([128, nh, d]), op=ALU.mult),
                  reads=["ro0", "ro1", "st3"], writes=[bk])
            ti = trrot.next()
            tk = trrot.key()
            for c in range(4):
                S.add("pe", lambda e, c=c: e.transpose(ptr[ti][:, c * 128:(c + 1) * 128], qb16[bi][:, c * 128:(c + 1) * 128], identB[:]),
                      reads=[bk, "identB"], writes=[tk])
            ki = ktrot.next()
            kk = ktrot.key()
            S.add("act", lambda e: e.activation(out=kT[ki][:], in_=ptr[ti][:], func=AF.Copy), reads=[tk], writes=[kk])
            if typ == "ka":
                dst = KTa[:, :, t * 128:(t + 1) * 128]
            elif typ == "kb":
                h0 = (c0 - 1024) // 128
                dst = KTb[h0:h0 + 4, :, t * 128:(t + 1) * 128]
            elif typ == "qa":
                h0 = (c0 - 3072) // 128
                dst = QTa[h0:h0 + 4, :, t * 128:(t + 1) * 128]
            else:
                h0 = (c0 - 5120) // 128
                dst = QTb[h0:h0 + 4, :, t * 128:(t + 1) * 128]
            S.add("sp", lambda e: e.dma_start(out=dst.rearrange("h d k -> d h k"),
                                              in_=kT[ki][:, :].rearrange("p (h k) -> p h k", h=4)),
                  reads=[kk], writes=[f"scr_{typ}_{c0}_{t}"], dma=True)

        def v_post(pb, t, typ, c0):
            pk = f"ppj{pb}"
            vi = vtrot.next()
            vk = vtrot.key()
            S.add("act", lambda e: e.activation(out=vt[vi][:], in_=ppj[pb][:], func=AF.Copy), reads=[pk], writes=[vk])
            if typ == "va":
                dst = Va[t * 128:(t + 1) * 128, :]
            else:
                c = c0 - 2048
                dst = Vb[t * 128:(t + 1) * 128, c:c + 512]
            S.add("sp", lambda e: e.dma_start(out=dst, in_=vt[vi][:]), reads=[vk], writes=[f"scr_{typ}_{c0}_{t}"], dma=True)

        def g_post(pb, fc, tb, typ):
            pk = f"ppj{pb}"
            vi = vtrot.next()
            vk = vtrot.key()
            S.add("act", lambda e: e.activation(out=vt[vi][:], in_=ppj[pb][:], func=AF.Sigmoid), reads=[pk], writes=[vk])
            dstT = sigA if typ == "ga" else sigB
            dst = dstT[fc * 128:(fc + 1) * 128, tb * 512:(tb + 1) * 512]
            S.add("sp", lambda e: e.dma_start(out=dst, in_=vt[vi][:]), reads=[vk], writes=[f"scr_{typ}_{fc}_{tb}"], dma=True)

        pend = [None]

        def flush_pend():
            if pend[0] is not None:
                pend[0]()
                pend[0] = None

        def project(tiles, groups):
            for (c0, typ, g) in groups:
                wb = wirot.next()
                wk = wirot.key()
                cast_load(wI[wb][:], w_in[:, c0:c0 + 512].rearrange("(kc p) n -> p kc n", p=128), wk)
                if typ in ("ga", "gb"):
                    flush_pend()
                    base = 6144 if typ == "ga" else 8192
                    for c4 in range(4):
                        fc = (c0 - base) // 128 + c4
                        for tb in range(len(tiles) // 4):
                            pb = prot.next()
                            for kc in range(KC):
                                S.add("pe", lambda e, pb=pb, wb=wb, kc=kc, c4=c4, tb=tb: e.matmul(
                                    ppj[pb][:], wI[wb][:, kc, c4 * 128:(c4 + 1) * 128], hT[:, kc, tb * 512:(tb + 1) * 512],
                                    start=(kc == 0), stop=(kc == KC - 1)),
                                    reads=[wk] + [f"hT{tb * 4 + i}" for i in range(4)], writes=[f"ppj{pb}"])
                            g_post(pb, fc, tb, typ)
                    continue
                for li, t in enumerate(tiles):
                    pb = prot.next()
                    for kc in range(KC):
                        S.add("pe", lambda e, pb=pb, wb=wb, kc=kc, li=li: e.matmul(
                            ppj[pb][:], hT[:, kc, li * 128:(li + 1) * 128], wI[wb][:, kc, :],
                            start=(kc == 0), stop=(kc == KC - 1)),
                            reads=[wk, f"hT{li}"], writes=[f"ppj{pb}"])
                    flush_pend()
                    if typ in ("va", "vb"):
                        pend[0] = lambda pb=pb, t=t, typ=typ, c0=c0: v_post(pb, t, typ, c0)
                    else:
                        pend[0] = lambda pb=pb, t=t, li=li, typ=typ, c0=c0: qk_post(pb, t, li, typ, c0)

        tilesA = list(range(16, 34))
        norm_tiles(tilesA, lambda t: 0 if t < 32 else 2)
        project(tilesA, IN_GROUPS[0:6])
        tilesB = list(range(0, 16))
        norm_tiles(tilesB, lambda t: 0)
        project(tilesB, IN_GROUPS)
        flush_pend()
        S.barrier()

    if debug == 1:
        return finish(nc, S, out_d, None)

    with contextlib.ExitStack() as ph2:
        OTa = ph2.enter_context(nc.sbuf_tensor("OTa", [128, 16, NQ], BF16))
        OTb = ph2.enter_context(nc.sbuf_tensor("OTb", [128, 8, NQ], BF16))
        if os.environ.get("DEBUG_MEMSET"):
            S.add("dve", lambda e: e.memset(OTa[:], 0.0), writes=["OTa_all"])
            S.add("dve", lambda e: e.memset(OTb[:], 0.0), writes=["OTb_all"])
        with contextlib.ExitStack() as ph:
            KTs = [ph.enter_context(nc.sbuf_tensor(f"KTs{i}", [128, NKEY], BF16)) for i in range(2)]
            Vs = [ph.enter_context(nc.sbuf_tensor(f"Vs{i}", [128, NT, 128], BF16)) for i in range(2)]
            QTs = [ph.enter_context(nc.sbuf_tensor(f"QTs{i}", [128, NQ], BF16)) for i in range(4)]
            racc = [ph.enter_context(nc.sbuf_tensor(f"racc{i}", [128, 512], F32)) for i in range(2)]
            rhi = ph.enter_context(nc.sbuf_tensor("rhi", [128, 512], BF16))
            rlo = ph.enter_context(nc.sbuf_tensor("rlo", [128, 512], BF16))
            rtmp = ph.enter_context(nc.sbuf_tensor("rtmp", [128, 512], F32))
            PT = [ph.enter_context(nc.sbuf_tensor(f"PT{i}", [128, 512], BF16)) for i in range(4)]
            rinv = [ph.enter_context(nc.sbuf_tensor(f"rinv{i}", [128, 512], F32)) for i in range(2)]
            t1 = [ph.enter_context(nc.sbuf_tensor(f"t1_{i}", [128, 512], F32)) for i in range(2)]
            ob = ph.enter_context(nc.sbuf_tensor("ob", [128, 512], F32))
            sqb = ph.enter_context(nc.sbuf_tensor("sqb", [128, 512], BF16))
            rs = ph.enter_context(nc.sbuf_tensor("rs", [128, 3, 512], F32))
            pS = [ph.enter_context(nc.psum_tensor(f"pS{i}", [128, 512], F32)) for i in range(3)]
            pO = [ph.enter_context(nc.psum_tensor(f"pO{i}", [128, 512], F32)) for i in range(2)]
            pR = [ph.enter_context(nc.psum_tensor(f"pR{i}", [128, 512], F32)) for i in range(2)]
            pN = ph.enter_context(nc.psum_tensor("pN", [128, 512], F32))

            kvrot = Rot("kv", 2)
            qrot = Rot("QTs", 4)
            srot = Rot("pS", 3)
            ptrot = Rot("PT", 4)

            blocks = []
            acc = 0
            for g in range(int(os.environ.get('ATT_A', '4'))):
                for hq in range(4):
                    for qb in range(4):
                        blocks.append(dict(kind="a", g=g, h=4 * g + hq, qb=qb, lo=0, hi=128, ai=acc % 2,
                                           scale=128 ** -0.5, newkv=(hq == 0 and qb == 0), newq=(qb == 0)))
                        acc += 1
            for h in range(int(os.environ.get('ATT_B', '8'))):
                for qb in range(4):
                    for sub in range(2):
                        blocks.append(dict(kind="b", h=h, qb=qb, sub=sub, lo=0, hi=128, ai=sub,
                                           scale=64 ** -0.5, newkv=(qb == 0 and sub == 0), newq=(qb == 0 and sub == 0)))
            cur = {"kvb": 0, "qi": 0, "qib": [0, 0]}

            def prefetch(blk):
                if blk["newkv"]:
                    kvb = kvrot.next()
                    cur["kvb"] = kvb
                    if blk["kind"] == "a":
                        g = blk["g"]
                        sp_load(KTs[kvb][:], KTa[g], f"KTs{kvb}")
                        sp_load(Vs[kvb][:], Va[:, g * 128:(g + 1) * 128].rearrange("(t p) d -> p t d", p=128), f"Vs{kvb}")
                    else:
                        h = blk["h"]
                        sp_load(KTs[kvb][:], KTb[h], f"KTs{kvb}")
                        sp_load(Vs[kvb][:], Vb[:, h * 128:(h + 1) * 128].rearrange("(t p) d -> p t d", p=128), f"Vs{kvb}")
                if blk["newq"]:
                    if blk["kind"] == "a":
                        qi = qrot.next()
                        cur["qi"] = qi
                        sp_load(QTs[qi][:], QTa[blk["h"]], f"QTs{qi}")
                    else:
                        for sub in range(2):
                            qi = qrot.next()
                            cur["qib"][sub] = qi
                            sp_load(QTs[qi][:], QTb[blk["h"]], f"QTs{qi}")
                            zlo = 64 if sub == 0 else 0
                            S.add("dve", lambda e, qi=qi, zlo=zlo: e.memset(QTs[qi][zlo:zlo + 64, :], 0.0),
                                  reads=[f"QTs{qi}"], writes=[f"QTs{qi}"])
                blk["kvb"] = cur["kvb"]
                blk["qi"] = cur["qi"] if blk["kind"] == "a" else cur["qib"][blk["sub"]]

            def post_a(blk):
                ai, h, qb = blk["ai"], blk["h"], blk["qb"]
                S.add("dve", lambda e: e.reciprocal(rinv[ai][:], pR[ai][:]), reads=[f"pR{ai}"], writes=[f"rinv{ai}"])
                S.add("dve", lambda e: e.tensor_tensor(OTa[:, h, qb * 512:(qb + 1) * 512], pO[ai][:], rinv[ai][:], op=ALU.mult),
                      reads=[f"pO{ai}", f"rinv{ai}"], writes=[f"OTa{h}_{qb}"])

            def post_b1(blk):
                sub = blk["sub"]
                S.add("dve", lambda e: e.reciprocal(rinv[sub][:], pR[sub][:]), reads=[f"pR{sub}"], writes=[f"rinv{sub}"])
                S.add("dve", lambda e: e.tensor_tensor(t1[sub][:], pO[sub][:], rinv[sub][:], op=ALU.mult),
                      reads=[f"pO{sub}", f"rinv{sub}"], writes=[f"t1_{sub}"])
                if sub == 1:
                    S.add("dve", lambda e: e.scalar_tensor_tensor(out=ob[:], in0=t1[1][:], scalar=lam[:, 0:1], in1=t1[0][:],
                                                                  op0=ALU.mult, op1=ALU.add),
                          reads=["t1_0", "t1_1"], writes=["ob"])
                    S.add("dve", lambda e: e.tensor_tensor(sqb[:], ob[:], ob[:], op=ALU.mult), reads=["ob"], writes=["sqb"])

            def post_b2(blk):
                S.add("pe", lambda e: e.matmul(pN[:], onesB[:], sqb[:], start=True, stop=True), reads=["sqb"], writes=["pN"])
                S.add("dve", lambda e: e.tensor_scalar(rs[:, 0, :], pN[:], 1.0 / 128, EPS, op0=ALU.mult, op1=ALU.add),
                      reads=["pN"], writes=["rs0"])

            def post_b3(blk):
                h, qb = blk["h"], blk["qb"]
                S.add("act", lambda e: e.activation(out=rs[:, 1, :], in_=rs[:, 0, :], func=AF.Sqrt), reads=["rs0"], writes=["rs1"])
                S.add("dve", lambda e: e.reciprocal(rs[:, 2, :], rs[:, 1, :]), reads=["rs1"], writes=["rs2"])
                S.add("dve", lambda e: e.tensor_tensor(ob[:], ob[:], rs[:, 2, :], op=ALU.mult), reads=["ob", "rs2"], writes=["ob"])
                S.add("dve", lambda e: e.tensor_scalar(OTb[:, h, qb * 512:(qb + 1) * 512], ob[:], lam[:, 1:2], None, op0=ALU.mult),
                      reads=["ob"], writes=[f"OTb{h}_{qb}"])

            items = [(bi, kt) for bi in range(len(blocks)) for kt in range(NT)]
            LOOK = 2
            sidx = {}
            deferred = []

            def issue_S(i):
                bi, kt = items[i]
                blk = blocks[bi]
                if kt == 0:
                    if bi == 0:
                        prefetch(blk)
                    if bi + 1 < len(blocks):
                        prefetch(blocks[bi + 1])
                si = srot.next()
                sidx[i] = si
                kvb, qi, lo, hi, qb = blk["kvb"], blk["qi"], blk["lo"], blk["hi"], blk["qb"]
                S.add("pe", lambda e: e.matmul(
                    pS[si][:], KTs[kvb][lo:hi, kt * 128:(kt + 1) * 128], QTs[qi][lo:hi, qb * 512:(qb + 1) * 512],
                    start=True, stop=True), reads=[f"KTs{kvb}", f"QTs{qi}"], writes=[f"pS{si}"])

            for i in range(min(LOOK, len(items))):
                issue_S(i)
            for i in range(len(items)):
                if i + LOOK < len(items):
                    issue_S(i + LOOK)
                bi, kt = items[i]
                blk = blocks[bi]
                si = sidx.pop(i)
                pi = ptrot.next()
                pk = ptrot.key()
                kvb, ai, scale = blk["kvb"], blk["ai"], blk["scale"]
                S.add("act", lambda e, si=si, pi=pi, scale=scale: e.activation(out=PT[pi][:], in_=pS[si][:], func=AF.Exp, scale=scale),
                      reads=[f"pS{si}"], writes=[pk])
                S.add("pe", lambda e, pi=pi, kt=kt, kvb=kvb, ai=ai: e.matmul(pO[ai][:], Vs[kvb][:, kt, :], PT[pi][:],
                                                                          start=(kt == 0), stop=(kt == NT - 1)),
                      reads=[f"Vs{kvb}", pk], writes=[f"pO{ai}"])
                if kt % 2 == 0:
                    S.add("pe", lambda e, pi=pi, kt=kt, ai=ai: e.matmul(pR[ai][:], onesB[:], PT[pi][:],
                                                                     start=(kt == 0), stop=False),
                          reads=[pk], writes=[f"pR{ai}"])
                elif kt == 1:
                    S.add("dve", lambda e, pi=pi, ai=ai: e.tensor_copy(racc[ai][:], PT[pi][:]), reads=[pk], writes=[f"racc{ai}"])
                else:
                    S.add("dve", lambda e, pi=pi, ai=ai: e.tensor_tensor(racc[ai][:], racc[ai][:], PT[pi][:], op=ALU.add),
                          reads=[pk, f"racc{ai}"], writes=[f"racc{ai}"])
                if kt == NT - 1:
                    S.add("dve", lambda e, ai=ai: e.tensor_copy(rhi[:], racc[ai][:]), reads=[f"racc{ai}"], writes=["rhi"])
                    S.add("dve", lambda e, ai=ai: e.tensor_tensor(rtmp[:], racc[ai][:], rhi[:], op=ALU.subtract),
                          reads=[f"racc{ai}", "rhi"], writes=["rtmp"])
                    S.add("dve", lambda e: e.tensor_copy(rlo[:], rtmp[:]), reads=["rtmp"], writes=["rlo"])
                for dd in [d for d in deferred if d[0] <= i]:
                    dd[1]()
                deferred = [d for d in deferred if d[0] > i]
                if kt == NT - 1:
                    def fin(blk=blk, ai=ai):
                        S.add("pe", lambda e: e.matmul(pR[ai][:], onesB[:], rhi[:], start=False, stop=False),
                              reads=["rhi"], writes=[f"pR{ai}"])
                        S.add("pe", lambda e: e.matmul(pR[ai][:], onesB[:], rlo[:], start=False, stop=True),
                              reads=["rlo"], writes=[f"pR{ai}"])
                        if blk["kind"] == "a":
                            post_a(blk)
                        else:
                            post_b1(blk)
                    deferred.append((i + 3, fin))
                    if blk["kind"] == "b" and blk["sub"] == 1:
                        deferred.append((i + 6, lambda blk=blk: post_b2(blk)))
                        deferred.append((i + 9, lambda blk=blk: post_b3(blk)))
            for dd in deferred:
                dd[1]()
            S.barrier()

        if debug == 2:
            return finish(nc, S, out_d, ("OT", OTa, OTb))

        with contextlib.ExitStack() as ph:
            mT = ph.enter_context(nc.sbuf_tensor("mT", [128, KC, 512], BF16))
            wBa = [ph.enter_context(nc.sbuf_tensor(f"wBa{i}", [128, 16, 256], BF16)) for i in range(2)]
            wBb = [ph.enter_context(nc.sbuf_tensor(f"wBb{i}", [128, 8, 256], BF16)) for i in range(2)]
            wO = [ph.enter_context(nc.sbuf_tensor(f"wO{i}", [128, KC, 512], BF16)) for i in range(2)]
            sg = [ph.enter_context(nc.sbuf_tensor(f"sg{i}", [128, 2, 512], BF16)) for i in range(3)]
            m1 = ph.enter_context(nc.sbuf_tensor("m1", [128, 512], F32))
            m2 = ph.enter_context(nc.sbuf_tensor("m2", [128, 512], F32))
            xp = [ph.enter_context(nc.sbuf_tensor(f"xp{i}", [128, 512], F32)) for i in range(3)]
            tmp = ph.enter_context(nc.sbuf_tensor("tmp3", [128, 512], F32))
            pA = [ph.enter_context(nc.psum_tensor(f"pA{i}", [128, 512], F32)) for i in range(2)]
            pB = [ph.enter_context(nc.psum_tensor(f"pB{i}", [128, 512], F32)) for i in range(2)]
            pX = [ph.enter_context(nc.psum_tensor(f"pX{i}", [128, 512], F32)) for i in range(2)]
            wbrot = Rot("wB", 2)
            worot = Rot("wO", 2)
            sgrot = Rot("sg", 3)
            xprot = Rot("xp", 3)
            for tb in range(4):
                for fg in range(8 if os.environ.get('P3A_BR', '1') == '1' else 0):
                    wb = wbrot.next()
                    cast_load(wBa[wb][:], w_br_a[:, fg * 256:(fg + 1) * 256].rearrange("(h p) n -> p h n", p=128), f"wBa{wb}")
                    cast_load(wBb[wb][:], w_br_b[:, fg * 256:(fg + 1) * 256].rearrange("(h p) n -> p h n", p=128), f"wBb{wb}")
                    for c2 in range(2):
                        fc = fg * 2 + c2
                        pi = fc % 2
                        si = sgrot.next()
                        sp_load(sg[si][:, 0, :], sigA[fc * 128:(fc + 1) * 128, tb * 512:(tb + 1) * 512], f"sgA{si}")
                        sp_load(sg[si][:, 1, :], sigB[fc * 128:(fc + 1) * 128, tb * 512:(tb + 1) * 512], f"sgB{si}")
                        for h in range(16):
                            S.add("pe", lambda e, pi=pi, wb=wb, h=h, c2=c2, tb=tb: e.matmul(
                                pA[pi][:], wBa[wb][:, h, c2 * 128:(c2 + 1) * 128], OTa[:, h, tb * 512:(tb + 1) * 512],
                                start=(h == 0), stop=(h == 15)), reads=[f"wBa{wb}"], writes=[f"pA{pi}"])
                        for h in range(8):
                            S.add("pe", lambda e, pi=pi, wb=wb, h=h, c2=c2, tb=tb: e.matmul(
                                pB[pi][:], wBb[wb][:, h, c2 * 128:(c2 + 1) * 128], OTb[:, h, tb * 512:(tb + 1) * 512],
                                start=(h == 0), stop=(h == 7)), reads=[f"wBb{wb}"], writes=[f"pB{pi}"])
                        S.add("dve", lambda e, pi=pi, si=si: e.tensor_tensor(m1[:], pA[pi][:], sg[si][:, 0, :], op=ALU.mult),
                              reads=[f"pA{pi}", f"sgA{si}"], writes=["m1"])
                        S.add("dve", lambda e, pi=pi, si=si: e.tensor_tensor(m2[:], pB[pi][:], sg[si][:, 1, :], op=ALU.mult),
                              reads=[f"pB{pi}", f"sgB{si}"], writes=["m2"])
                        S.add("dve", lambda e, fc=fc: e.tensor_tensor(mT[:, fc, :], m1[:], m2[:], op=ALU.add),
                              reads=["m1", "m2"], writes=[f"mT{fc}"])
                for fg in range(4 if os.environ.get('P3A_OUT', '1') == '1' else 0):
                    wo = worot.next()
                    cast_load(wO[wo][:], w_out[:, fg * 512:(fg + 1) * 512].rearrange("(kc p) n -> p kc n", p=128), f"wO{wo}")
                    for tt in range(4):
                        t = tb * 4 + tt
                        xi = xprot.next()
                        sp_load(xp[xi][:], xall[t * 128:(t + 1) * 128, fg * 512:(fg + 1) * 512], f"xp{xi}")
                        pi = (fg * 4 + tt) % 2
                        for kc in range(KC):
                            S.add("pe", lambda e, pi=pi, wo=wo, kc=kc, tt=tt: e.matmul(
                                pX[pi][:], mT[:, kc, tt * 128:(tt + 1) * 128], wO[wo][:, kc, :],
                                start=(kc == 0), stop=(kc == KC - 1)),
                                reads=[f"wO{wo}", f"mT{kc}"], writes=[f"pX{pi}"])
                        S.add("dve", lambda e, pi=pi, fg=fg: e.tensor_tensor(tmp[:], pX[pi][:], gbc[:, 0, fg * 512:(fg + 1) * 512], op=ALU.mult),
                              reads=[f"pX{pi}"], writes=["tmp3"])
                        S.add("dve", lambda e, xi=xi: e.tensor_tensor(xp[xi][:], xp[xi][:], tmp[:], op=ALU.add),
                              reads=["tmp3", f"xp{xi}"], writes=[f"xp{xi}"])
                        S.add("sp", lambda e, xi=xi, t=t, fg=fg: e.dma_start(out=xnew_d[t * 128:(t + 1) * 128, fg * 512:(fg + 1) * 512], in_=xp[xi][:]),
                              reads=[f"xp{xi}"], writes=[f"xnew_{t}_{fg}"], dma=True)
            S.barrier()

    if debug == 3:
        return finish(nc, S, out_d, None)

    with contextlib.ExitStack() as ph:
        aT = ph.enter_context(nc.sbuf_tensor("aT", [128, NFF, 512], BF16))
        h2T = ph.enter_context(nc.sbuf_tensor("h2T", [128, KC, 512], BF16))
        xw = [ph.enter_context(nc.sbuf_tensor(f"xw{i}", [128, D], F32)) for i in range(2)]
        xn2 = ph.enter_context(nc.sbuf_tensor("xn2", [128, D], F32))
        junk2 = ph.enter_context(nc.sbuf_tensor("junk2", [128, D], BF16))
        st2 = ph.enter_context(nc.sbuf_tensor("st2", [128, 4], F32))
        wG = [ph.enter_context(nc.sbuf_tensor(f"wG{i}", [128, KC, 256], BF16)) for i in range(2)]
        wU = [ph.enter_context(nc.sbuf_tensor(f"wU{i}", [128, KC, 256], BF16)) for i in range(2)]
        wD = [ph.enter_context(nc.sbuf_tensor(f"wD{i}", [128, NFF // 2, 512], BF16)) for i in range(2)]
        sgl = [ph.enter_context(nc.sbuf_tensor(f"sgl{i}", [128, 512], F32)) for i in range(2)]
        xr = [ph.enter_context(nc.sbuf_tensor(f"xr{i}", [128, 512], F32)) for i in range(8)]
        tmp4 = ph.enter_context(nc.sbuf_tensor("tmp4", [128, 512], F32))
        pst2 = [ph.enter_context(nc.psum_tensor(f"pst2_{i}", [128, 512], F32)) for i in range(2)]
        pG = [ph.enter_context(nc.psum_tensor(f"pG{i}", [128, 512], F32)) for i in range(2)]
        pU = [ph.enter_context(nc.psum_tensor(f"pU{i}", [128, 512], F32)) for i in range(2)]
        xwrot = Rot("xw", 2)
        gurot = Rot("wGU", 2)
        wdrot = Rot("wD", 2)
        xrrot = Rot("xr", 8)
        slrot = Rot("sgl", 2)
        out_ops = []
        for tb in range(4):
            for tt in range(4):
                t = tb * 4 + tt
                b = xwrot.next()
                xk = xwrot.key()
                S.add("sp", lambda e, b=b, t=t: e.dma_start(out=xw[b][:], in_=xnew_d[t * 128:(t + 1) * 128, :]),
                      reads=[f"xnew_{t}_{fg}" for fg in range(4)], writes=[xk], dma=True)
                S.add("act", lambda e, b=b: e.activation(out=junk2[:], in_=xw[b][:], func=AF.Square, accum_out=st2[:, 0:1]),
                      reads=[xk], writes=["junk2", "s0"])
                S.add("dve", lambda e: e.tensor_scalar(st2[:, 1:2], st2[:, 0:1], 1.0 / D, EPS, op0=ALU.mult, op1=ALU.add),
                      reads=["s0"], writes=["s1"])
                S.add("act", lambda e: e.activation(out=st2[:, 2:3], in_=st2[:, 1:2], func=AF.Sqrt), reads=["s1"], writes=["s2"])
                S.add("dve", lambda e: e.reciprocal(st2[:, 3:4], st2[:, 2:3]), reads=["s2"], writes=["s3"])
                S.add("dve", lambda e, b=b: e.tensor_scalar(xn2[:], xw[b][:], st2[:, 3:4], None, op0=ALU.mult),
                      reads=["s3", xk], writes=["xn2"])
                for kc in range(KC):
                    pb = (kc // 4) % 2
                    S.add("pe", lambda e, kc=kc, pb=pb: e.transpose(pst2[pb][:, (kc % 4) * 128:(kc % 4 + 1) * 128],
                                                                    xn2[:, kc * 128:(kc + 1) * 128], identF[:]),
                          reads=["xn2", "identF"], writes=[f"pst2_{pb}"])
                    S.add("dve", lambda e, kc=kc, pb=pb, tt=tt: e.tensor_scalar(
                        h2T[:, kc, tt * 128:(tt + 1) * 128], pst2[pb][:, (kc % 4) * 128:(kc % 4 + 1) * 128],
                        GB[:, 4, kc:kc + 1], GB[:, 5, kc:kc + 1], op0=ALU.mult, op1=ALU.add),
                        reads=[f"pst2_{pb}"], writes=[f"h2T{tt}"])
            for fg in range(22):
                gb_ = gurot.next()
                cast_load(wG[gb_][:], w_gate[:, fg * 256:(fg + 1) * 256].rearrange("(kc p) n -> p kc n", p=128), f"wG{gb_}")
                cast_load(wU[gb_][:], w_up[:, fg * 256:(fg + 1) * 256].rearrange("(kc p) n -> p kc n", p=128), f"wU{gb_}")
                for c2 in range(2):
                    ffc = fg * 2 + c2
                    pi = ffc % 2
                    for kc in range(KC):
                        S.add("pe", lambda e, pi=pi, gb_=gb_, kc=kc, c2=c2: e.matmul(
                            pG[pi][:], wG[gb_][:, kc, c2 * 128:(c2 + 1) * 128], h2T[:, kc, :],
                            start=(kc == 0), stop=(kc == KC - 1)),
                            reads=[f"wG{gb_}"] + [f"h2T{i}" for i in range(4)], writes=[f"pG{pi}"])
                    for kc in range(KC):
                        S.add("pe", lambda e, pi=pi, gb_=gb_, kc=kc, c2=c2: e.matmul(
                            pU[pi][:], wU[gb_][:, kc, c2 * 128:(c2 + 1) * 128], h2T[:, kc, :],
                            start=(kc == 0), stop=(kc == KC - 1)),
                            reads=[f"wU{gb_}"] + [f"h2T{i}" for i in range(4)], writes=[f"pU{pi}"])
                    sl = slrot.next()
                    S.add("act", lambda e, pi=pi, sl=sl: e.activation(out=sgl[sl][:], in_=pG[pi][:], func=AF.Silu),
                          reads=[f"pG{pi}"], writes=[f"sgl{sl}"])
                    S.add("dve", lambda e, pi=pi, sl=sl, ffc=ffc: e.tensor_tensor(aT[:, ffc, :], pU[pi][:], sgl[sl][:], op=ALU.mult),
                          reads=[f"pU{pi}", f"sgl{sl}"], writes=[f"aT{ffc}"])
            accs = [("pG0", pG[0]), ("pG1", pG[1]), ("pU0", pU[0]), ("pU1", pU[1])]
            HK = NFF // 2
            for fg in range(4):
                xis = []
                for half in range(2):
                    wd = wdrot.next()
                    cast_load(wD[wd][:], w_down[half * HK * 128:(half + 1) * HK * 128, fg * 512:(fg + 1) * 512]
                              .rearrange("(c p) n -> p c n", p=128), f"wD{wd}")
                    for tt in range(4):
                        t = tb * 4 + tt
                        if half == 0:
                            xi = xrrot.next()
                            xis.append(xi)
                            S.add("sp", lambda e, xi=xi, t=t, fg=fg: e.dma_start(out=xr[xi][:], in_=xnew_d[t * 128:(t + 1) * 128, fg * 512:(fg + 1) * 512]),
                                  writes=[f"xr{xi}"], dma=True)
                        ak, at = accs[tt]
                        for c in range(HK):
                            cc = half * HK + c
                            S.add("pe", lambda e, at=at, wd=wd, c=c, cc=cc, tt=tt: e.matmul(
                                at[:], aT[:, cc, tt * 128:(tt + 1) * 128], wD[wd][:, c, :],
                                start=(cc == 0), stop=(cc == NFF - 1)),
                                reads=[f"wD{wd}", f"aT{cc}"], writes=[ak])
                for tt in range(4):
                    t = tb * 4 + tt
                    xi = xis[tt]
                    ak, at = accs[tt]
                    S.add("dve", lambda e, at=at, fg=fg: e.tensor_tensor(tmp4[:], at[:], gbc[:, 1, fg * 512:(fg + 1) * 512], op=ALU.mult),
                          reads=[ak], writes=["tmp4"])
                    S.add("dve", lambda e, xi=xi: e.tensor_tensor(xr[xi][:], xr[xi][:], tmp4[:], op=ALU.add),
                          reads=["tmp4", f"xr{xi}"], writes=[f"xr{xi}"])
                    out_ops.append(S.add("sp", lambda e, xi=xi, t=t, fg=fg: e.dma_start(
                        out=out_d[t * 128:(t + 1) * 128, fg * 512:(fg + 1) * 512], in_=xr[xi][:]),
                        reads=[f"xr{xi}"], writes=[f"out_{t}_{fg}"], dma=True))
    return finish(nc, S, out_d, None)


def finish(nc, S, out_d, extra):
    fin = Op("sp", None, False)
    fin.idx = S.nops
    for o in S.pending_dma:
        fin.deps[o] = "raw"
    if S.fence is not None:
        fin.deps[S.fence] = "raw"
    S.ops["sp"].append(fin)
    S.finalize()
    with contextlib.ExitStack() as st:
        sems = {e: st.enter_context(nc.semaphore(f"s_{e}")) for e in ENGS}
        dsems = [st.enter_context(nc.semaphore(f"d_{i}")) for i in range(S.n_dma_sems)]
        block = st.enter_context(nc.Block())

        @block.tensor
        def _(e):
            S.emit_one("pe", e, sems, dsems)

        @block.scalar
        def _(e):
            S.emit_one("act", e, sems, dsems)

        @block.vector
        def _(e):
            S.emit_one("dve", e, sems, dsems)

        @block.gpsimd
        def _(e):
            S.emit_one("pool", e, sems, dsems)

        @block.sync
        def _(e):
            S.emit_one("sp", e, sems, dsems)
    return nc


def _rope_tables(n_tokens_tile_positions):
    pos = n_tokens_tile_positions.astype(np.float32)
    row = np.floor(pos / GRID_W).astype(np.float32)
    col = (pos - row * GRID_W).astype(np.float32)
    outs = []
    for hd in (128, 64):
        nf = hd // 4
        freqs = (10000.0 ** (-np.arange(nf, dtype=np.float32) / nf)).astype(np.float32)
        ang = np.concatenate([row[:, None] * freqs, col[:, None] * freqs], axis=-1).astype(np.float32)
        outs.append((np.cos(ang).astype(np.float32), np.sin(ang).astype(np.float32)))
    return outs


def make_in_maps(x, c, ctx, c_ctx, w_ada, b_ada, norm1_g, w_in, q_norm_a, k_norm_a, q_norm_b, k_norm_b,
                 lam_q1, lam_k1, lam_q2, lam_k2, subln_g, w_br_a, w_br_b, w_out, norm2_g,
                 w_ff_gate, w_ff_up, w_ff_down):
    f = lambda a: np.ascontiguousarray(np.asarray(a, dtype=np.float32))
    x = f(x); c = f(c); ctx = f(ctx); c_ctx = f(c_ctx)
    shared = {
        "w_ada": f(w_ada)[0], "w_in": f(w_in)[0], "w_br_a": f(w_br_a)[0], "w_br_b": f(w_br_b)[0],
        "w_out": f(w_out)[0], "w_ff_gate": f(w_ff_gate)[0], "w_ff_up": f(w_ff_up)[0], "w_ff_down": f(w_ff_down)[0],
        "ident": np.eye(128, dtype=np.float32),
    }
    fm = lambda v: np.ascontiguousarray(f(v).reshape(-1, 128).T)
    vecs = np.concatenate([fm(norm1_g[0]), fm(norm2_g[0]), fm(b_ada[0]), f(subln_g[0]).reshape(128, 1)], axis=1)
    gains_row = np.concatenate([np.tile(f(q_norm_a[0]), 4), np.tile(f(k_norm_a[0]), 4),
                                np.tile(f(q_norm_b[0]), 8), np.tile(f(k_norm_b[0]), 8)])
    gains = np.ascontiguousarray(np.broadcast_to(gains_row[None, :], (128, 2048)))
    lam_row = np.concatenate([f(lam_q1[0]), f(lam_k1[0]), f(lam_q2[0]), f(lam_k2[0])])
    lamv = np.ascontiguousarray(np.broadcast_to(lam_row[None, :], (128, 256)))
    shared.update({"vecs": np.ascontiguousarray(vecs), "gains": gains, "lamv": lamv})
    (cA, sA), (cB, sB) = _rope_tables(np.arange(4096))
    rope_lat = np.concatenate([cA, sA, cB, sB], axis=1).astype(np.float32)
    rope_ctx = np.concatenate([np.ones((256, 64)), np.zeros((256, 64)), np.ones((256, 32)), np.zeros((256, 32))],
                              axis=1).astype(np.float32)
    in_maps = []
    for core in range(8):
        b, h = core // 2, core % 2
        own = slice(h * 2048, (h + 1) * 2048)
        oth = slice((1 - h) * 2048, (2 - h) * 2048)
        xall = np.concatenate([x[b, own], x[b, oth], ctx[b]], axis=0)
        rope = np.concatenate([rope_lat[own], rope_lat[oth], rope_ctx], axis=0)
        cpair = np.stack([c[b], c_ctx], axis=0)
        cT = np.ascontiguousarray(cpair.reshape(2, 16, 128).transpose(2, 1, 0).reshape(128, 32))
        m = dict(shared)
        m.update({"xall": np.ascontiguousarray(xall), "rope": np.ascontiguousarray(rope), "cT": cT})
        in_maps.append(m)
    return in_maps


_NC_CACHE = {}


def kernel(**inputs):
    in_maps = make_in_maps(**inputs)
    if "nc" not in _NC_CACHE:
        _NC_CACHE["nc"] = build_program(0)
    nc = _NC_CACHE["nc"]
    res = run_bass_kernel_spmd(nc, in_maps, core_ids=list(range(8)))
    out = np.zeros((4, 4096, 2048), dtype=np.float32)
    for core in range(8):
        b, h = core // 2, core % 2
        out[b, h * 2048:(h + 1) * 2048, :] = res.results[core]["out"]
    return out
```
